# Optimizing a Trainium2 kernel written in Bass

```python
import math
import jax, jax.numpy as jnp
from jax import lax
import numpy as np

D_MODEL = 1024
BATCH = 4
SEQ = 8192
DEPTH = 2

N_MIXERS = 2
N_CONV_LAYERS = (DEPTH + 1) // 2
N_ATTN_LAYERS = DEPTH // 2
D_FF = 2816
CONV_WIDTH = 3
HEAD_DIM = 64
N_HEADS = D_MODEL // HEAD_DIM
DILATION_GROUPS = ((128, 1), (512, 4), (2048, 16))
N_GROUPS = len(DILATION_GROUPS)
BLOCK = 128
N_BUCKETS = 32
MAX_EXACT = N_BUCKETS // 2
MAX_DISTANCE = 2048
RMS_EPS = 1e-6
NEG_INF = -1e30

kernel_name = "hybrid_shortconv_dilated_swa_macaron"


def rmsnorm(x, g):
    xf = x.astype(jnp.float32)
    y = xf * lax.rsqrt(jnp.mean(xf * xf, axis=-1, keepdims=True) + RMS_EPS) * g.astype(jnp.float32)
    return y.astype(x.dtype)


def swiglu(h, w_in, w_out):
    gate, up = jnp.split(h @ w_in, 2, axis=-1)
    return (jax.nn.silu(gate) * up) @ w_out


def short_conv_mixer(h, w_in, w_conv, w_out):
    b_gate, c_gate, u = jnp.split(h @ w_in, 3, axis=-1)
    v = c_gate * u
    T = v.shape[1]
    vp = jnp.pad(v, ((0, 0), (CONV_WIDTH - 1, 0), (0, 0)))
    conv = sum(w_conv[lag] * vp[:, CONV_WIDTH - 1 - lag:CONV_WIDTH - 1 - lag + T]
               for lag in range(CONV_WIDTH))
    return (b_gate * conv) @ w_out


def t5_bucket(dist):
    is_small = dist < MAX_EXACT
    nf = jnp.maximum(dist, 1).astype(jnp.float32)
    large = MAX_EXACT + (jnp.log(nf / MAX_EXACT) / math.log(MAX_DISTANCE / MAX_EXACT)
                         * (N_BUCKETS - MAX_EXACT)).astype(jnp.int32)
    large = jnp.minimum(large, N_BUCKETS - 1)
    return jnp.where(is_small, dist, large)


def band_bias(rel_bias_g, dilation, n_steps):
    i = jnp.arange(BLOCK)[:, None]
    j = jnp.arange(2 * BLOCK)[None, :]
    step = BLOCK + i - j
    in_band = (step >= 0) & (step <= n_steps)
    bucket = t5_bucket(jnp.clip(step, 0, n_steps) * dilation)
    bias = rel_bias_g[bucket]
    return jnp.transpose(bias, (2, 0, 1)).astype(jnp.float32), in_band


def dilated_group(q, k, v, bias, in_band, dilation):
    B, T, H, Dh = q.shape
    span = dilation * BLOCK
    Tp = -(-T // span) * span
    L = Tp // dilation
    Lb = L // BLOCK

    def to_sub(a):
        a = jnp.pad(a, ((0, 0), (0, Tp - T), (0, 0), (0, 0))).reshape(B, L, dilation, H, Dh)
        return jnp.transpose(a, (0, 2, 1, 3, 4)).reshape(B, dilation, Lb, BLOCK, H, Dh)

    def with_prev(a):
        prev = jnp.pad(a, ((0, 0), (0, 0), (1, 0), (0, 0), (0, 0), (0, 0)))[:, :, :-1]
        return jnp.concatenate([prev, a], axis=3)

    qs = to_sub(q)
    kb = with_prev(to_sub(k))
    vb = with_prev(to_sub(v))
    logits = jnp.einsum('bsnqhd,bsnkhd->bsnhqk', qs, kb) + bias
    n_idx = jnp.arange(Lb)[:, None, None, None]
    j_idx = jnp.arange(2 * BLOCK)[None, None, None, :]
    valid = in_band[None, None] & ~((n_idx == 0) & (j_idx < BLOCK))
    logits = jnp.where(valid, logits, NEG_INF)
    m = jnp.max(logits, axis=-1, keepdims=True)
    p = jnp.exp(logits - m)
    s = jnp.sum(p, axis=-1, keepdims=True)
    o = jnp.einsum('bsnhqk,bsnkhd->bsnqhd', p, vb)
    o = o / jnp.transpose(s, (0, 1, 2, 4, 3, 5))
    lse = jnp.transpose((m + jnp.log(s))[..., 0], (0, 1, 2, 4, 3))

    def from_sub(a):
        rest = a.shape[4:]
        a = a.reshape((B, dilation, L) + rest)
        a = jnp.moveaxis(a, 1, 2).reshape((B, Tp) + rest)
        return a[:, :T]

    return from_sub(o), from_sub(lse)


def dilated_attention_mixer(h, w_qkv, q_gain, k_gain, w_out, rel_bias):
    B, T, _ = h.shape
    qkv = (h @ w_qkv).reshape(B, T, N_GROUPS, 3, N_HEADS, HEAD_DIM)
    outs, lses = [], []
    for g, (window, dilation) in enumerate(DILATION_GROUPS):
        q = rmsnorm(qkv[:, :, g, 0].astype(jnp.float32), q_gain[g]) * (HEAD_DIM ** -0.5)
        k = rmsnorm(qkv[:, :, g, 1].astype(jnp.float32), k_gain[g])
        v = qkv[:, :, g, 2].astype(jnp.float32)
        bias, in_band = band_bias(rel_bias[:, g * N_HEADS:(g + 1) * N_HEADS], dilation, window // dilation)
        o, lse = dilated_group(q, k, v, bias, in_band, dilation)
        outs.append(o)
        lses.append(lse)
    wts = jax.nn.softmax(jnp.stack(lses), axis=0)
    o = jnp.einsum('gbth,gbthd->bthd', wts, jnp.stack(outs))
    return o.reshape(B, T, N_HEADS * HEAD_DIM).astype(h.dtype) @ w_out


def setup_inputs(seed: int = 0) -> dict:
    key = jax.random.key(seed)
    ks = jax.random.split(key, 16)
    f32 = jnp.float32

    def w(k, shape, fan_in):
        return jax.random.normal(k, shape, f32) * (fan_in ** -0.5)

    def gain(k, shape):
        return 1.0 + 0.05 * jax.random.normal(k, shape, f32)

    return {
        "x": jax.random.normal(ks[0], (BATCH, SEQ, D_MODEL), f32),
        "norm_ffn1": gain(ks[1], (DEPTH, D_MODEL)),
        "ffn1_w_in": w(ks[2], (DEPTH, D_MODEL, 2 * D_FF), D_MODEL),
        "ffn1_w_out": w(ks[3], (DEPTH, D_FF, D_MODEL), D_FF),
        "norm_mix": gain(ks[4], (DEPTH, D_MODEL)),
        "conv_w_in": w(ks[5], (N_CONV_LAYERS, D_MODEL, 3 * D_MODEL), D_MODEL),
        "conv_w": w(ks[6], (N_CONV_LAYERS, CONV_WIDTH, D_MODEL), CONV_WIDTH),
        "conv_w_out": w(ks[7], (N_CONV_LAYERS, D_MODEL, D_MODEL), D_MODEL),
        "attn_w_qkv": w(ks[8], (N_ATTN_LAYERS, D_MODEL, N_GROUPS * 3 * N_HEADS * HEAD_DIM), D_MODEL),
        "attn_q_gain": gain(ks[9], (N_ATTN_LAYERS, N_GROUPS, HEAD_DIM)),
        "attn_k_gain": gain(ks[10], (N_ATTN_LAYERS, N_GROUPS, HEAD_DIM)),
        "attn_w_out": w(ks[11], (N_ATTN_LAYERS, N_HEADS * HEAD_DIM, D_MODEL), N_HEADS * HEAD_DIM),
        "rel_bias": 0.5 * jax.random.normal(ks[12], (N_BUCKETS, N_GROUPS * N_HEADS), f32),
        "norm_ffn2": gain(ks[13], (DEPTH, D_MODEL)),
        "ffn2_w_in": w(ks[14], (DEPTH, D_MODEL, 2 * D_FF), D_MODEL),
        "ffn2_w_out": w(ks[15], (DEPTH, D_FF, D_MODEL), D_FF),
    }


def reference(x, norm_ffn1, ffn1_w_in, ffn1_w_out, norm_mix, conv_w_in, conv_w, conv_w_out,
              attn_w_qkv, attn_q_gain, attn_k_gain, attn_w_out, rel_bias,
              norm_ffn2, ffn2_w_in, ffn2_w_out):
    h = x
    for i in range(DEPTH):
        h = h + 0.5 * swiglu(rmsnorm(h, norm_ffn1[i]), ffn1_w_in[i], ffn1_w_out[i])
        hn = rmsnorm(h, norm_mix[i])
        j = i // N_MIXERS
        if i % N_MIXERS == 0:
            mix = short_conv_mixer(hn, conv_w_in[j], conv_w[j], conv_w_out[j])
        else:
            mix = dilated_attention_mixer(hn, attn_w_qkv[j], attn_q_gain[j], attn_k_gain[j],
                                          attn_w_out[j], rel_bias)
        h = h + mix
        h = h + 0.5 * swiglu(rmsnorm(h, norm_ffn2[i]), ffn2_w_in[i], ffn2_w_out[i])
    return h
```

```python
import contextlib
import math
import numpy as np
import concourse.bass as bass
import concourse.mybir as mybir
from concourse.bass_utils import run_bass_kernel_spmd

F32 = mybir.dt.float32
BF16 = mybir.dt.bfloat16
AF = mybir.ActivationFunctionType
ALU = mybir.AluOpType

ENGS = ("tensor", "scalar", "vector", "gpsimd", "sync")

D = 1024
DFF = 2816
NB = DFF // 128
SPAN = 2048
EXT = 2
W = SPAN + EXT
NCORES = 8
EPS = 1e-6
GROUPS = ((128, 1), (512, 4), (2048, 16))

STOP_AFTER = None
SAME_ENGINE_SYNC = True


class Op:
    __slots__ = ("eng", "fn", "deps", "signal", "eidx", "sigval", "waits", "dma_sem", "dma_val", "gidx")

    def __init__(self, eng, fn):
        self.eng = eng
        self.fn = fn
        self.deps = []
        self.signal = False
        self.eidx = -1
        self.sigval = -1
        self.waits = []
        self.dma_sem = None
        self.dma_val = 0
        self.gidx = -1


class Sched:
    def __init__(self, nc):
        self.nc = nc
        self.ops = []
        self.eng_ops = {e: [] for e in ENGS}
        self.tiles = {}
        self.dma_cnt = {}

    def op(self, eng, fn, reads=(), writes=(), dma_sem=None):
        o = Op(eng, fn)
        o.gidx = len(self.ops)
        o.eidx = len(self.eng_ops[eng])
        if dma_sem is not None:
            k = id(dma_sem)
            self.dma_cnt[k] = self.dma_cnt.get(k, 0) + 16
            o.dma_sem = dma_sem
            o.dma_val = self.dma_cnt[k]
        deps = {}
        for k in reads:
            st = self.tiles.get(k)
            if st is not None and st[0] is not None:
                deps[st[0].gidx] = st[0]
        for k in writes:
            st = self.tiles.get(k)
            if st is not None:
                if st[0] is not None:
                    deps[st[0].gidx] = st[0]
                for r in st[1]:
                    deps[r.gidx] = r
        o.deps = [deps[g] for g in sorted(deps)]
        for k in reads:
            st = self.tiles.setdefault(k, [None, []])
            st[1].append(o)
        for k in writes:
            self.tiles[k] = [o, []]
        self.ops.append(o)
        self.eng_ops[eng].append(o)
        return o

    def barrier(self, dummy):
        keys = list(self.tiles.keys())
        self.op("vector", lambda e: e.memset(dummy, 0.0), writes=keys + ["__bar"])
        for e in ENGS:
            if e != "vector":
                self.op(e, lambda eng: eng.nop(), reads=["__bar"])

    def finalize(self, sems):
        seen_idx = {e: {x: -1 for x in ENGS} for e in ENGS}
        seen_dma = {e: {} for e in ENGS}
        for o in self.ops:
            e = o.eng
            for d in o.deps:
                if d.dma_sem is not None:
                    k = id(d.dma_sem)
                    if seen_dma[e].get(k, 0) >= d.dma_val:
                        continue
                    seen_dma[e][k] = d.dma_val
                    o.waits.append(d)
                else:
                    if d.eng == e and (e == "tensor" or not SAME_ENGINE_SYNC):
                        continue
                    if seen_idx[e][d.eng] >= d.eidx:
                        continue
                    seen_idx[e][d.eng] = d.eidx
                    d.signal = True
                    o.waits.append(d)
        for o in self.ops:
            best = {}
            for d in o.waits:
                if d.dma_sem is not None:
                    k = id(d.dma_sem)
                    if k not in best or best[k].dma_val < d.dma_val:
                        best[k] = d
            o.waits = [d for d in o.waits if d.dma_sem is None or best[id(d.dma_sem)] is d]
        for e in ENGS:
            c = 0
            for o in self.eng_ops[e]:
                if o.dma_sem is None and o.signal:
                    c += 1
                    o.sigval = c
        self.sems = sems

    def emit_engine(self, e, engobj):
        sems = self.sems
        for o in self.eng_ops[e]:
            for d in o.waits:
                if d.dma_sem is not None:
                    engobj.wait_ge(d.dma_sem, d.dma_val)
                else:
                    engobj.wait_ge(sems[d.eng], d.sigval)
            ins = o.fn(engobj)
            if o.dma_sem is not None:
                ins.then_inc(o.dma_sem, 16)
            elif o.signal:
                ins.then_inc(sems[e], 1)

    def emit(self, block):
        s = self

        @block.tensor
        def _(eng):
            s.emit_engine("tensor", eng)

        @block.scalar
        def _(eng):
            s.emit_engine("scalar", eng)

        @block.vector
        def _(eng):
            s.emit_engine("vector", eng)

        @block.gpsimd
        def _(eng):
            s.emit_engine("gpsimd", eng)

        @block.sync
        def _(eng):
            s.emit_engine("sync", eng)


def subkeys(name, idx, c0, n):
    ks = []
    c = c0
    while c < c0 + n:
        s = min(c // 512, 4)
        ks.append((name, idx, s))
        c = (c // 512 + 1) * 512
    return ks


def kx(p0, n):
    return [("kx", c) for c in range(p0 // 512, (p0 + n - 1) // 512 + 1)]


MAIN_SUBS = [(0, 512), (512, 512), (1024, 512), (1536, 512)]
EXT_SUB = (2048, 2)


def build_program():
    nc = bass.Bass("TRN2", target_bir_lowering=False)
    dt = nc.dram_tensor
    xh = dt("xh", [D, W], F32, kind="ExternalInput")
    xo = dt("xo", [D, 2 * SPAN], F32, kind="ExternalInput")
    flag_d = dt("flag", [128, 1], F32, kind="ExternalInput")
    gains_d = dt("gains", [128, 48], F32, kind="ExternalInput")
    fw_in = dt("fw_in", [4, NB, 128, 2048], F32, kind="ExternalInput")
    fw_out = dt("fw_out", [4, DFF, D], F32, kind="ExternalInput")
    cw_in = dt("cw_in", [8, 128, 3072], F32, kind="ExternalInput")
    cw_d = dt("cw", [128, 24], F32, kind="ExternalInput")
    cw_out = dt("cw_out", [D, D], F32, kind="ExternalInput")
    aw_qkv = dt("aw_qkv", [3, 8, 128, 3072], F32, kind="ExternalInput")
    aw_out = dt("aw_out", [D, D], F32, kind="ExternalInput")
    gqk_d = dt("gqk", [128, 6], F32, kind="ExternalInput")
    relb_d = dt("relb", [32, 48], F32, kind="ExternalInput")
    oneh_d = dt("onehot", [32, 3 * 129], F32, kind="ExternalInput")
    outT = dt("outT", [D, 2 * SPAN], F32, kind="ExternalOutput")
    wr_dram = dt("wr_dram", [48, 384], F32)
    kvK = dt("kvK", [2, 3, 8, 128, 2048], BF16)
    kvV = dt("kvV", [2, 3, 8, 128, 16 * 256], BF16)

    with contextlib.ExitStack() as es:
        E = es.enter_context
        hT = E(nc.sbuf_tensor("hT", [128, 8, W], F32))
        xn = E(nc.sbuf_tensor("xn", [128, 8, W], BF16))
        vext = E(nc.sbuf_tensor("vext", [128, 32, 256], BF16))
        sq = E(nc.sbuf_tensor("sq", [128, 4, 512], BF16))
        rstd = E(nc.sbuf_tensor("rstd", [128, 2, 512], F32))
        ones32 = E(nc.sbuf_tensor("ones32", [128, 128], BF16))
        bd32 = E(nc.sbuf_tensor("bd32", [128, 128], BF16))
        gains = E(nc.sbuf_tensor("gains_sb", [128, 48], F32))
        flag = E(nc.sbuf_tensor("flag_sb", [128, 1], F32))
        cwv = E(nc.sbuf_tensor("cw_sb", [128, 24], F32))
        gqk = E(nc.sbuf_tensor("gqk_sb", [128, 6], F32))
        relb = E(nc.sbuf_tensor("relb_sb", [32, 48], F32))
        vhalo = E(nc.sbuf_tensor("vhalo", [128, 8, 2], F32))
        dummy = E(nc.sbuf_tensor("dummy_sb", [128, 8], F32))
        ARENA_F32 = 18500
        arena = E(nc.sbuf_tensor("arena", [128, ARENA_F32], F32))
        ps = E(nc.psum_tensor("ps", [128, 8, 512], F32))
        sems = {e: E(nc.semaphore("s_" + e)) for e in ENGS}
        nsem = [0]

        def new_sem():
            nsem[0] += 1
            return E(nc.semaphore("d%d" % nsem[0]))

        block = E(nc.Block())
        S = Sched(nc)

        def carve(off_bytes, shape, dtype):
            esz = 2 if dtype == BF16 else 4
            n = int(np.prod(shape[1:]))
            assert off_bytes % 4 == 0
            a = arena[:, off_bytes // 4: off_bytes // 4 + (n * esz + 3) // 4]
            if dtype == BF16:
                a = a.bitcast(BF16)
            a = a[:, 0:n]
            if len(shape) > 2:
                names = "abcde"[:len(shape) - 1]
                pat = "p (" + " ".join(names) + ") -> p " + " ".join(names)
                a = a.rearrange(pat, **{names[i]: shape[1 + i] for i in range(len(shape) - 2)})
            return a, off_bytes + ((n * esz + 63) // 64) * 64

        bank_ctr = [0]

        def bank():
            b = bank_ctr[0] % 8
            bank_ctr[0] += 1
            return b

        oneh_full, _o = carve(0, [128, 3 * 129], F32)
        oneh = oneh_full[0:32, :]
        wr_full, _o = carve(_o, [128, 3, 384], F32)
        wr_sb = wr_full[0:16, :, :]
        sc = new_sem()
        S.op("sync", lambda e: e.dma_start(out=gains[:, :], in_=gains_d[:, :]), writes=["gains"], dma_sem=sc)
        S.op("sync", lambda e: e.dma_start(out=flag[:, :], in_=flag_d[:, :]), writes=["flag"], dma_sem=sc)
        S.op("sync", lambda e: e.dma_start(out=cwv[:, :], in_=cw_d[:, :]), writes=["cwv"], dma_sem=sc)
        S.op("sync", lambda e: e.dma_start(out=gqk[:, :], in_=gqk_d[:, :]), writes=["gqk"], dma_sem=sc)
        S.op("sync", lambda e: e.dma_start(out=relb[:, :], in_=relb_d[:, :]), writes=["relb"], dma_sem=sc)
        S.op("sync", lambda e: e.dma_start(out=oneh[:, :], in_=oneh_d[:, :]), writes=["oneh"], dma_sem=sc)
        S.op("vector", lambda e: e.memset(ones32[:, :], 1.0), writes=["ones32"])
        S.op("vector", lambda e: e.memset(bd32[:, :], 0.0), writes=["bd32"])
        S.op("vector", lambda e: e.memset(bd32[0:64, 0:64], 1.0), writes=["bd32"])
        S.op("vector", lambda e: e.memset(bd32[64:128, 64:128], 1.0), writes=["bd32"])
        S.op("vector", lambda e: e.memset(vext[:, :, :], 1.0), writes=["vext"])
        S.op("vector", lambda e: e.memset(wr_sb[:, :, :], 0.0), writes=["wr_sb"])
        S.op("vector", lambda e: e.memset(vhalo[:, :, :], 0.0), writes=["vhalo"])
        S.barrier(dummy[:, :])
        b0 = bank()
        for g in range(3):
            S.op("tensor", lambda e, g=g: e.matmul(ps[0:16, b0, g * 129:(g + 1) * 129], lhsT=relb[:, g * 16:(g + 1) * 16],
                                                  rhs=oneh[:, g * 129:(g + 1) * 129], start=True, stop=True),
                 reads=["relb", "oneh"], writes=[("ps", b0)])
            S.op("scalar", lambda e, g=g: e.activation(out=wr_sb[:, g, 127:256], in_=ps[0:16, b0, g * 129:(g + 1) * 129], func=AF.Exp),
                 reads=[("ps", b0)], writes=["wr_sb"])
        for g in range(3):
            S.op("sync", lambda e, g=g: e.dma_start(out=wr_dram[g * 16:(g + 1) * 16, :], in_=wr_sb[:, g, :]),
                 reads=["wr_sb"], writes=["wr_dram"], dma_sem=sc)
        S.barrier(dummy[:, :])

        def rmsnorm(gi, subs):
            for si, (c0, n) in enumerate(subs):
                ss = bank()
                for kc in range(8):
                    j = kc % 4
                    if kc % 2 == 0:
                        S.op("scalar", lambda e, kc=kc, j=j, c0=c0, n=n: e.activation(out=sq[:, j, 0:n], in_=hT[:, kc, c0:c0 + n], func=AF.Square),
                             reads=subkeys("h", kc, c0, n), writes=[("sq", j)])
                    else:
                        S.op("gpsimd", lambda e, kc=kc, j=j, c0=c0, n=n: e.tensor_tensor(out=sq[:, j, 0:n], in0=hT[:, kc, c0:c0 + n], in1=hT[:, kc, c0:c0 + n], op=ALU.mult),
                             reads=subkeys("h", kc, c0, n), writes=[("sq", j)])
                    S.op("tensor", lambda e, kc=kc, j=j, n=n, ss=ss: e.matmul(ps[:, ss, 0:n], lhsT=ones32[:, :], rhs=sq[:, j, 0:n],
                                                                             start=(kc == 0), stop=(kc == 7)),
                         reads=[("sq", j), "ones32"], writes=[("ps", ss)])
                r = si % 2
                S.op("scalar", lambda e, r=r, n=n, ss=ss: e.activation(out=rstd[:, r, 0:n], in_=ps[:, ss, 0:n], func=AF.Ln, scale=1.0 / D, bias=EPS),
                     reads=[("ps", ss)], writes=[("rstd", r)])
                S.op("scalar", lambda e, r=r, n=n: e.activation(out=rstd[:, r, 0:n], in_=rstd[:, r, 0:n], func=AF.Exp, scale=-0.5),
                     reads=[("rstd", r)], writes=[("rstd", r)])
                for kc in range(8):
                    S.op("vector", lambda e, kc=kc, r=r, c0=c0, n=n: e.scalar_tensor_tensor(
                        out=xn[:, kc, c0:c0 + n], in0=hT[:, kc, c0:c0 + n], scalar=gains[:, gi * 8 + kc:gi * 8 + kc + 1],
                        in1=rstd[:, r, 0:n], op0=ALU.mult, op1=ALU.mult),
                         reads=subkeys("h", kc, c0, n) + [("rstd", r), "gains"], writes=subkeys("xn", kc, c0, n))

        def ffn(f, gi, subs):
            rmsnorm(gi, subs)
            off = 0
            hid, off = carve(off, [128, 6, W], BF16)
            wi, off = carve(off, [128, 3, 2048], BF16)
            wo, off = carve(off, [128, 2, 6, 1024], BF16)
            sg, off = carve(off, [128, 2, 512], F32)
            assert off <= ARENA_F32 * 4, off
            groups = [(0, 6), (6, 12), (12, 17), (17, 22)]
            for gidx, (bA, bB) in enumerate(groups):
                par = gidx % 2
                for b in range(bA, bB):
                    bl = b - bA
                    sl = b % 3
                    S.op("gpsimd", lambda e, sl=sl, b=b: e.dma_start(out=wi[:, sl, :], in_=fw_in[f, b, :, :]),
                         writes=[("wi", sl)], dma_sem=ffn_wi_sems[sl])
                    S.op("gpsimd", lambda e, par=par, bl=bl, b=b: e.dma_start(out=wo[:, par, bl, :], in_=fw_out[f, b * 128:(b + 1) * 128, :]),
                         writes=[("wo", par, bl)], dma_sem=ffn_wo_sems[par])
                    for (c0, n) in subs:
                        pg = bank()
                        pu = bank()

                        def mm_gate(e, sl=sl, c0=c0, n=n, pg=pg):
                            for kc in range(8):
                                ins = e.matmul(ps[:, pg, 0:n], lhsT=wi[:, sl, kc * 256:kc * 256 + 128], rhs=xn[:, kc, c0:c0 + n],
                                               start=(kc == 0), stop=(kc == 7))
                            return ins

                        def mm_up(e, sl=sl, c0=c0, n=n, pu=pu):
                            for kc in range(8):
                                ins = e.matmul(ps[:, pu, 0:n], lhsT=wi[:, sl, kc * 256 + 128:kc * 256 + 256], rhs=xn[:, kc, c0:c0 + n],
                                               start=(kc == 0), stop=(kc == 7))
                            return ins

                        xk = [k for kc in range(8) for k in subkeys("xn", kc, c0, n)]
                        S.op("tensor", mm_gate, reads=[("wi", sl)] + xk, writes=[("ps", pg)])
                        S.op("tensor", mm_up, reads=[("wi", sl)] + xk, writes=[("ps", pu)])
                        j = pg % 2
                        S.op("scalar", lambda e, j=j, n=n, pg=pg: e.activation(out=sg[:, j, 0:n], in_=ps[:, pg, 0:n], func=AF.Silu),
                             reads=[("ps", pg)], writes=[("sg", j)])
                        S.op("vector", lambda e, j=j, n=n, pu=pu, bl=bl, c0=c0: e.tensor_tensor(
                            out=hid[:, bl, c0:c0 + n], in0=sg[:, j, 0:n], in1=ps[:, pu, 0:n], op=ALU.mult),
                             reads=[("sg", j), ("ps", pu)], writes=subkeys("hid", bl, c0, n))
                nbl = bB - bA
                for d in range(8):
                    for (c0, n) in subs:
                        py = bank()

                        def mm_out(e, par=par, nbl=nbl, d=d, c0=c0, n=n, py=py):
                            for bl in range(nbl):
                                ins = e.matmul(ps[:, py, 0:n], lhsT=wo[:, par, bl, d * 128:(d + 1) * 128], rhs=hid[:, bl, c0:c0 + n],
                                               start=(bl == 0), stop=(bl == nbl - 1))
                            return ins

                        S.op("tensor", mm_out, reads=[("wo", par, bl) for bl in range(nbl)] + [k for bl in range(nbl) for k in subkeys("hid", bl, c0, n)],
                             writes=[("ps", py)])
                        S.op("vector", lambda e, d=d, c0=c0, n=n, py=py: e.scalar_tensor_tensor(
                            out=hT[:, d, c0:c0 + n], in0=ps[:, py, 0:n], scalar=0.5, in1=hT[:, d, c0:c0 + n], op0=ALU.mult, op1=ALU.add),
                             reads=[("ps", py)] + subkeys("h", d, c0, n), writes=subkeys("h", d, c0, n))
            S.barrier(dummy[:, :])

        def conv_mixer(gi, subs, span_kind):
            rmsnorm(gi, subs)
            off = 0
            wc, off = carve(off, [128, 2, 3072], BF16)
            wco, off = carve(off, [128, 2, 1024], BF16)
            c_sb, off = carve(off, [128, 2, 512], F32)
            b_sb, off = carve(off, [128, SPAN], F32)
            v_e, off = carve(off, [128, W], F32)
            acc, off = carve(off, [128, SPAN], F32)
            z, off = carve(off, [128, 2, SPAN], BF16)
            assert off <= ARENA_F32 * 4, off
            pend_out = [None]
            for fb in range(8):
                sl = fb % 2
                S.op("gpsimd", lambda e, sl=sl, fb=fb: e.dma_start(out=wc[:, sl, 0:1536], in_=cw_in[fb, :, 0:1536]),
                     writes=[("wc", sl)], dma_sem=cv_sems[sl])
                S.op("gpsimd", lambda e, sl=sl, fb=fb: e.dma_start(out=wc[:, sl, 1536:3072], in_=cw_in[fb, :, 1536:3072]),
                     writes=[("wcB", sl)], dma_sem=cv_sems[sl])
                S.op("gpsimd", lambda e, sl=sl, fb=fb: e.dma_start(out=wco[:, sl, :], in_=cw_out[fb * 128:(fb + 1) * 128, :]),
                     writes=[("wco", sl)], dma_sem=cv_sems[2 + sl])
                if span_kind == "S0":
                    S.op("vector", lambda e, fb=fb: e.tensor_scalar(out=v_e[:, 0:2], in0=vhalo[:, fb, :], scalar1=flag[:, 0:1], scalar2=None, op0=ALU.mult),
                         reads=[("vhalo", fb), "flag"], writes=[("v_e", 5)])
                elif span_kind == "S1":
                    S.op("vector", lambda e, fb=fb: e.tensor_copy(out=v_e[:, 0:2], in_=vhalo[:, fb, :]),
                         reads=[("vhalo", fb)], writes=[("v_e", 5)])
                for (c0, n) in subs:
                    is_ext = c0 >= SPAN
                    xk = [k for kc in range(8) for k in subkeys("xn", kc, c0, n)]
                    pc = bank()
                    pu = bank()

                    def mm(e, sl=sl, c0=c0, n=n, pb=None, col=0):
                        for kc in range(8):
                            ins = e.matmul(ps[:, pb, 0:n], lhsT=wc[:, sl, kc * 384 + col:kc * 384 + col + 128], rhs=xn[:, kc, c0:c0 + n],
                                           start=(kc == 0), stop=(kc == 7))
                        return ins

                    S.op("tensor", lambda e, pc=pc, mm=mm: mm(e, pb=pc, col=128), reads=[("wc", sl), ("wcB", sl)] + xk, writes=[("ps", pc)])
                    S.op("tensor", lambda e, pu=pu, mm=mm: mm(e, pb=pu, col=256), reads=[("wc", sl), ("wcB", sl)] + xk, writes=[("ps", pu)])
                    cj = pc % 2
                    S.op("scalar", lambda e, n=n, pc=pc, cj=cj: e.activation(out=c_sb[:, cj, 0:n], in_=ps[:, pc, 0:n], func=AF.Copy),
                         reads=[("ps", pc)], writes=[("c_sb", cj)])
                    vc0 = 0 if is_ext else 2 + c0
                    vkey = ("v_e", 5) if is_ext else ("v_e", c0 // 512)
                    S.op("vector", lambda e, n=n, pu=pu, vc0=vc0, cj=cj: e.tensor_tensor(out=v_e[:, vc0:vc0 + n], in0=c_sb[:, cj, 0:n], in1=ps[:, pu, 0:n], op=ALU.mult),
                         reads=[("c_sb", cj), ("ps", pu)], writes=[vkey])
                    if not is_ext:
                        pb = bank()
                        S.op("tensor", lambda e, pb=pb, mm=mm: mm(e, pb=pb, col=0), reads=[("wc", sl), ("wcB", sl)] + xk, writes=[("ps", pb)])
                        S.op("scalar", lambda e, n=n, pb=pb, c0=c0: e.activation(out=b_sb[:, c0:c0 + n], in_=ps[:, pb, 0:n], func=AF.Copy),
                             reads=[("ps", pb)], writes=[("b_sb", c0 // 512)])
                vall = [("v_e", s) for s in range(4)] + [("v_e", 5)]
                ball = [("b_sb", s) for s in range(4)]
                S.op("vector", lambda e, fb=fb: e.tensor_scalar(out=acc[:, :], in0=v_e[:, 2:2 + SPAN], scalar1=cwv[:, fb * 3:fb * 3 + 1], scalar2=None, op0=ALU.mult),
                     reads=vall + ["cwv"], writes=["acc"])
                for lag in (1, 2):
                    S.op("vector", lambda e, fb=fb, lag=lag: e.scalar_tensor_tensor(
                        out=acc[:, :], in0=v_e[:, 2 - lag:2 - lag + SPAN], scalar=cwv[:, fb * 3 + lag:fb * 3 + lag + 1], in1=acc[:, :],
                        op0=ALU.mult, op1=ALU.add), reads=vall + ["cwv", "acc"], writes=["acc"])
                S.op("vector", lambda e, sl=sl: e.tensor_tensor(out=z[:, sl, :], in0=b_sb[:, :], in1=acc[:, :], op=ALU.mult),
                     reads=ball + ["acc"], writes=[("z", sl)])
                S.op("vector", lambda e, fb=fb: e.tensor_copy(out=vhalo[:, fb, :], in_=v_e[:, SPAN:SPAN + 2]),
                     reads=vall, writes=[("vhalo", fb)])
                def outproj(sl=sl):
                    for d in range(8):
                        for (c0, n) in MAIN_SUBS:
                            py = bank()
                            S.op("tensor", lambda e, sl=sl, d=d, c0=c0, n=n, py=py: e.matmul(
                                ps[:, py, 0:n], lhsT=wco[:, sl, d * 128:(d + 1) * 128], rhs=z[:, sl, c0:c0 + n], start=True, stop=True),
                                 reads=[("wco", sl), ("z", sl)], writes=[("ps", py)])
                            S.op("vector", lambda e, d=d, c0=c0, n=n, py=py: e.tensor_tensor(
                                out=hT[:, d, c0:c0 + n], in0=ps[:, py, 0:n], in1=hT[:, d, c0:c0 + n], op=ALU.add),
                                 reads=[("ps", py)] + subkeys("h", d, c0, n), writes=subkeys("h", d, c0, n))

                if pend_out[0] is not None:
                    pend_out[0]()
                pend_out[0] = outproj
            pend_out[0]()
            S.barrier(dummy[:, :])

        def attn_mixer(gi, span_kind, par_prev, par_cur):
            own = span_kind != "H"
            store = span_kind != "S1"
            rmsnorm(gi, MAIN_SUBS)
            off = 0
            wq, off = carve(off, [128, 2, 3072], BF16)
            wao, off = carve(off, [128, 2, 1024], BF16)
            kext, off = carve(off, [128, 2 * SPAN], BF16)
            qT, off = carve(off, [128, SPAN], BF16)
            acc, off = carve(off, [128, 2, SPAN], F32)
            rden, off = carve(off, [128, 2, 512], F32)
            ao, off = carve(off, [128, 2, SPAN], BF16)
            ebr, off = carve(off, [128, 2, 2, 128], F32)
            ebf, off = carve(off, [128, 2, 2, 2, 128], F32)
            exs, off = carve(off, [128, 2, 512], F32)
            pT, off = carve(off, [128, 2, 512], BF16)
            assert off <= ARENA_F32 * 4, off
            ctr = [0, 0, 0, 0]
            bg = []

            def bg_pop(k):
                for _ in range(min(k, len(bg))):
                    bg.pop(0)()

            def load_wq(it_):
                if it_ >= 24:
                    return
                hp_, g_, sl_ = it_ // 3, it_ % 3, it_ % 2
                S.op("gpsimd", lambda e: e.dma_start(out=wq[:, sl_, 0:1536], in_=aw_qkv[g_, hp_, :, 0:1536]),
                     writes=[("wq", sl_)], dma_sem=at_sems[sl_])
                S.op("gpsimd", lambda e: e.dma_start(out=wq[:, sl_, 1536:3072], in_=aw_qkv[g_, hp_, :, 1536:3072]),
                     writes=[("wqB", sl_)], dma_sem=at_sems[sl_])

            for hp in range(8):
                for g in range(3):
                    P, r = GROUPS[g]
                    nbk = 16 // r
                    it = ctr[0]
                    ctr[0] += 1
                    sl = it % 2
                    if it == 0:
                        load_wq(0)
                    load_wq(it + 1)
                    if own:
                        S.op("sync", lambda e, g=g, hp=hp, P=P: e.dma_start(out=kext[:, 0:P], in_=kvK[par_prev, g, hp, :, 0:P]),
                             reads=[("kvK", par_prev, g, hp)], writes=kx(0, P), dma_sem=at_sems[2])
                        vsrc = kvV[par_prev, g, hp, :, 0:r * 256].rearrange("p (s c) -> p s c", s=r)
                        vdst = vext[:, 0:r * (nbk + 1), :].rearrange("p (s n) c -> p s n c", s=r)[:, :, 0, :]
                        S.op("sync", lambda e, vsrc=vsrc, vdst=vdst: e.dma_start(out=vdst, in_=vsrc),
                             reads=[("kvV", par_prev, g, hp)], writes=[("vext", s_ * (nbk + 1)) for s_ in range(r)], dma_sem=at_sems[3])
                        src = bass.AP(wr_dram, (g * 16 + hp * 2) * 384, [[1, 128], [384, 2], [128, 2], [1, 128]])
                        S.op("sync", lambda e, src=src: e.dma_start(out=ebr[:, :, :, :], in_=src),
                             reads=["wr_dram"], writes=["ebr"], dma_sem=at_sems[4])
                        for h in range(2):
                            for c in range(2):
                                rev = bass.AP(ebr.tensor, ebr[:, h, c, :].offset + 127, [list(ebr[:, h, c, :].ap[0]), [-1, 128]])
                                S.op("vector", lambda e, h=h, c=c, rev=rev: e.tensor_copy(out=ebf[:, 1, h, c, :], in_=rev),
                                     reads=["ebr"], writes=["ebf"])
                                if c == 0:
                                    S.op("vector", lambda e, h=h, c=c, rev=rev: e.tensor_scalar(
                                        out=ebf[:, 0, h, c, :], in0=rev, scalar1=flag[:, 0:1], scalar2=None, op0=ALU.mult),
                                         reads=["ebr", "flag"], writes=["ebf"])
                                else:
                                    S.op("vector", lambda e, h=h, c=c, rev=rev: e.tensor_copy(out=ebf[:, 0, h, c, :], in_=rev),
                                         reads=["ebr"], writes=["ebf"])
                    if own:
                        tsubs = MAIN_SUBS
                    else:
                        tsubs = [(SPAN - P, P)] if P <= 512 else MAIN_SUBS
                    state = {"pend": None, "pendb": None}

                    def head(task):
                        which, c0, n = task
                        xk = [k for kc in range(8) for k in subkeys("xn", kc, c0, n)]
                        pq = bank()

                        def mmqk(e, sl=sl, c0=c0, n=n, pq=pq, which=which):
                            for kc in range(8):
                                ins = e.matmul(ps[:, pq, 0:n], lhsT=wq[:, sl, kc * 384 + which * 128:kc * 384 + which * 128 + 128],
                                               rhs=xn[:, kc, c0:c0 + n], start=(kc == 0), stop=(kc == 7))
                            return ins

                        S.op("tensor", mmqk, reads=[("wq", sl), ("wqB", sl)] + xk, writes=[("ps", pq)])
                        j = ctr[2] % 4
                        ctr[2] += 1
                        S.op("scalar", lambda e, j=j, n=n, pq=pq: e.activation(out=sq[:, j, 0:n], in_=ps[:, pq, 0:n], func=AF.Square),
                             reads=[("ps", pq)], writes=[("sq", j)])
                        return (task, pq, j)

                    def tail(st):
                        (which, c0, n), pq, j = st
                        pss = bank()
                        S.op("tensor", lambda e, j=j, n=n, pss=pss: e.matmul(ps[:, pss, 0:n], lhsT=bd32[:, :], rhs=sq[:, j, 0:n], start=True, stop=True),
                             reads=[("sq", j), "bd32"], writes=[("ps", pss)])
                        rr = ctr[3] % 2
                        ctr[3] += 1
                        S.op("scalar", lambda e, rr=rr, n=n, pss=pss: e.activation(out=rstd[:, rr, 0:n], in_=ps[:, pss, 0:n], func=AF.Ln, scale=1.0 / 64, bias=EPS),
                             reads=[("ps", pss)], writes=[("rstd", rr)])
                        S.op("scalar", lambda e, rr=rr, n=n: e.activation(out=rstd[:, rr, 0:n], in_=rstd[:, rr, 0:n], func=AF.Exp, scale=-0.5),
                             reads=[("rstd", rr)], writes=[("rstd", rr)])
                        if which == 1:
                            dst = kext[:, P + c0:P + c0 + n]
                            dkey = kx(P + c0, n)
                        else:
                            dst = qT[:, c0:c0 + n]
                            dkey = subkeys("qT", 0, c0, n)
                        gcol = g * 2 + (1 if which == 1 else 0)
                        S.op("vector", lambda e, rr=rr, n=n, pq=pq, dst=dst, gcol=gcol: e.scalar_tensor_tensor(
                            out=dst, in0=ps[:, pq, 0:n], scalar=gqk[:, gcol:gcol + 1],
                            in1=rstd[:, rr, 0:n], op0=ALU.mult, op1=ALU.mult),
                             reads=[("ps", pq), ("rstd", rr), "gqk"], writes=dkey)

                    def flush_tail():
                        if state["pend"] is not None:
                            tail(state["pend"])
                            state["pend"] = None

                    def proj(task):
                        st = head(task)
                        flush_tail()
                        state["pend"] = st

                    def vgroup(grp):
                        pv = bank()
                        for gi_, (s, nn) in enumerate(grp):
                            m0 = (128 * (nn - 1)) * r + s
                            lo, hi = m0, m0 + 127 * r + 1

                            def mmv(e, sl=sl, m0=m0, pv=pv, gi_=gi_, r=r):
                                for kc in range(8):
                                    ins = e.matmul(ps[:, pv, gi_ * 128:(gi_ + 1) * 128], lhsT=xn[:, kc, m0:m0 + 127 * r + 1:r],
                                                   rhs=wq[:, sl, kc * 384 + 256:kc * 384 + 384], start=(kc == 0), stop=(kc == 7))
                                return ins

                            S.op("tensor", mmv, reads=[("wq", sl), ("wqB", sl)] + [k for kc in range(8) for k in subkeys("xn", kc, lo, hi - lo)], writes=[("ps", pv)])
                        flush_tail()
                        L = len(grp)
                        tixs = [s * (nbk + 1) + nn for (s, nn) in grp]
                        st_ = (tixs[1] - tixs[0]) if L > 1 else 1
                        assert all(tixs[i] == tixs[0] + i * st_ for i in range(L))
                        vsel = vext[:, tixs[0]:tixs[0] + st_ * (L - 1) + 1:st_, :]
                        psv = ps[:, pv, 0:L * 128].rearrange("p (t c) -> p t c", c=128)
                        S.op("vector", lambda e, vsel=vsel, psv=psv: e.tensor_copy(out=vsel[:, :, 0:64], in_=psv[:, :, 0:64]),
                             reads=[("ps", pv)], writes=[("vext", t) for t in tixs])
                        S.op("vector", lambda e, vsel=vsel, psv=psv: e.tensor_copy(out=vsel[:, :, 192:256], in_=psv[:, :, 64:128]),
                             reads=[("ps", pv)], writes=[("vext", t) for t in tixs])

                    def blk_head(s, nn):
                        if bank_ctr[0] % 8 == 7:
                            bank_ctr[0] += 1
                        pS0 = bank()
                        pS1 = bank()
                        assert pS1 == pS0 + 1
                        q0 = (128 * (nn - 1)) * r + s
                        e0, e1 = 128 * (nn - 1) * r, 128 * (nn + 1) * r
                        kkeys = kx(e0, e1 - e0)
                        qkeys = subkeys("qT", 0, 128 * (nn - 1) * r, 128 * r)

                        def mms(e, pS0=pS0, pS1=pS1, q0=q0, r=r, nn=nn, s=s):
                            for h in range(2):
                                for c in range(2):
                                    k0 = (128 * (nn - 1 + c)) * r + s
                                    ins = e.matmul(ps[:, (pS0, pS1)[h], c * 128:(c + 1) * 128],
                                                   lhsT=kext[h * 64:(h + 1) * 64, k0:k0 + 127 * r + 1:r],
                                                   rhs=qT[h * 64:(h + 1) * 64, q0:q0 + 127 * r + 1:r], start=True, stop=True)
                            return ins

                        S.op("tensor", mms, reads=kkeys + qkeys, writes=[("ps", pS0), ("ps", pS1)])
                        j = ctr[1] % 2
                        ctr[1] += 1
                        S.op("scalar", lambda e, j=j, pS0=pS0: e.activation(
                            out=exs[:, j, :].rearrange("p (h x) -> p h x", h=2), in_=ps[:, pS0:pS0 + 2, 0:256], func=AF.Exp, scale=0.125),
                             reads=[("ps", pS0), ("ps", pS1)], writes=[("exs", j)])
                        var = 0 if (nn == 1 and span_kind == "S0") else 1
                        meng = "vector"
                        S.op(meng, lambda e, j=j, var=var: e.tensor_tensor(
                            out=pT[:, j, :], in0=exs[:, j, :], in1=ebf[:, var, :, :, :].rearrange("p h c i -> p (h c i)"), op=ALU.mult),
                             reads=[("exs", j), "ebf"], writes=[("pT", j)])
                        return (s, nn, j, q0)

                    def blk_tail(st):
                        s, nn, j, q0 = st
                        pO = bank()

                        def mmo(e, j=j, pO=pO, nn=nn, s=s, nbk=nbk):
                            for h in range(2):
                                for c in range(2):
                                    tix = s * (nbk + 1) + (nn - 1 + c)
                                    ins = e.matmul(ps[:, pO, h * 128:(h + 1) * 128], lhsT=vext[:, tix, h * 128:(h + 1) * 128],
                                                   rhs=pT[:, j, (h * 2 + c) * 128:(h * 2 + c + 1) * 128], start=(c == 0), stop=(c == 1))
                            return ins

                        S.op("tensor", mmo, reads=[("pT", j), ("vext", s * (nbk + 1) + nn - 1), ("vext", s * (nbk + 1) + nn)], writes=[("ps", pO)])
                        adst = acc[:, :, q0:q0 + 127 * r + 1:r]
                        asrc = ps[:, pO, 0:256].rearrange("p (h i) -> p h i", h=2)
                        if g == 0:
                            S.op("scalar", lambda e, adst=adst, asrc=asrc: e.activation(out=adst, in_=asrc, func=AF.Copy),
                                 reads=[("ps", pO)], writes=["acc"])
                        else:
                            S.op("vector", lambda e, adst=adst, asrc=asrc: e.tensor_tensor(out=adst, in0=asrc, in1=adst, op=ALU.add),
                                 reads=[("ps", pO), "acc"], writes=["acc"])

                    def blocks(lst):
                        for (s, nn) in lst:
                            stb = blk_head(s, nn)
                            if state["pendb"] is not None:
                                blk_tail(state["pendb"])
                            state["pendb"] = stb

                    def flush_blocks():
                        if state["pendb"] is not None:
                            blk_tail(state["pendb"])
                            state["pendb"] = None

                    def spill():
                        if store:
                            S.op("sync", lambda e, g=g, hp=hp, P=P: e.dma_start(out=kvK[par_cur, g, hp, :, 0:P], in_=kext[:, SPAN:SPAN + P]),
                                 reads=kx(SPAN, P), writes=[("kvK", par_cur, g, hp)], dma_sem=at_sems[5])
                            vdst = kvV[par_cur, g, hp, :, 0:r * 256].rearrange("p (s c) -> p s c", s=r)
                            vsrc = vext[:, 0:r * (nbk + 1), :].rearrange("p (s n) c -> p s n c", s=r)[:, :, nbk, :]
                            S.op("sync", lambda e, vsrc=vsrc, vdst=vdst: e.dma_start(out=vdst, in_=vsrc),
                                 reads=[("vext", s_ * (nbk + 1) + nbk) for s_ in range(r)], writes=[("kvV", par_cur, g, hp)], dma_sem=at_sems[6])

                    if not own:
                        for (c0, n) in tsubs:
                            proj((1, c0, n))
                        vt = [(s, nbk) for s in range(r)]
                        for ti in range(0, len(vt), 4):
                            vgroup(vt[ti:ti + 4])
                        flush_tail()
                        spill()
                        continue
                    if g < 2:
                        def punit(k):
                            c0, n = MAIN_SUBS[k]
                            proj((1, c0, n))
                            proj((0, c0, n))
                            if g == 0:
                                vgroup([(0, nn) for nn in range(4 * k + 1, 4 * k + 5)])
                            else:
                                vgroup([(s, k + 1) for s in range(4)])

                        def bunit(k):
                            if g == 0:
                                blocks([(0, nn) for nn in range(4 * k + 1, 4 * k + 5)])
                            else:
                                blocks([(s, k + 1) for s in range(4)])
                            bg_pop(3)

                        punit(0)
                        punit(1)
                        bunit(0)
                        punit(2)
                        bunit(1)
                        punit(3)
                        flush_tail()
                        bunit(2)
                        bunit(3)
                        flush_blocks()
                        spill()
                    else:
                        for (c0, n) in MAIN_SUBS:
                            proj((1, c0, n))
                            proj((0, c0, n))
                        vt = [(s, 1) for s in range(r)]
                        for ti in range(0, len(vt), 4):
                            vgroup(vt[ti:ti + 4])
                        flush_tail()
                        for ti in range(0, 16, 4):
                            blocks([(s, 1) for s in range(ti, ti + 4)])
                            bg_pop(3)
                        flush_blocks()
                        spill()
                if not own:
                    continue
                asl = hp % 2
                S.op("gpsimd", lambda e, asl=asl, hp=hp: e.dma_start(out=wao[:, asl, :], in_=aw_out[hp * 128:(hp + 1) * 128, :]),
                     writes=[("wao", asl)], dma_sem=at_sems[7 + asl])
                for q in range(4):
                    cs = slice(q * 512, (q + 1) * 512)
                    rq = q % 2
                    S.op("scalar", lambda e, cs=cs, rq=rq: e.activation(out=rden[0:64, rq, :], in_=acc[64:128, 0, cs], func=AF.Ln), reads=["acc"], writes=[("rden", rq)])
                    S.op("scalar", lambda e, cs=cs, rq=rq: e.activation(out=rden[64:128, rq, :], in_=acc[0:64, 1, cs], func=AF.Ln), reads=["acc"], writes=[("rden", rq)])
                    S.op("scalar", lambda e, rq=rq: e.activation(out=rden[:, rq, :], in_=rden[:, rq, :], func=AF.Exp, scale=-1.0), reads=[("rden", rq)], writes=[("rden", rq)])
                    S.op("vector", lambda e, asl=asl, cs=cs, rq=rq: e.tensor_tensor(out=ao[0:64, asl, cs], in0=acc[0:64, 0, cs], in1=rden[0:64, rq, :], op=ALU.mult),
                         reads=["acc", ("rden", rq)], writes=[("ao", asl, q)])
                    S.op("vector", lambda e, asl=asl, cs=cs, rq=rq: e.tensor_tensor(out=ao[64:128, asl, cs], in0=acc[64:128, 1, cs], in1=rden[64:128, rq, :], op=ALU.mult),
                         reads=["acc", ("rden", rq)], writes=[("ao", asl, q)])
                for qi, (c0, n) in enumerate(MAIN_SUBS):
                    for d in range(8):
                        def oproj(asl=asl, d=d, c0=c0, n=n, qi=qi):
                            py = bank()
                            S.op("tensor", lambda e: e.matmul(
                                ps[:, py, 0:n], lhsT=wao[:, asl, d * 128:(d + 1) * 128], rhs=ao[:, asl, c0:c0 + n], start=True, stop=True),
                                 reads=[("wao", asl), ("ao", asl, qi)], writes=[("ps", py)])
                            S.op("vector", lambda e: e.tensor_tensor(
                                out=hT[:, d, c0:c0 + n], in0=ps[:, py, 0:n], in1=hT[:, d, c0:c0 + n], op=ALU.add),
                                 reads=[("ps", py)] + subkeys("h", d, c0, n), writes=subkeys("h", d, c0, n))
                        bg.append(oproj)
            bg_pop(len(bg))
            S.barrier(dummy[:, :])

        ffn_wi_sems = [new_sem() for _ in range(3)]
        ffn_wo_sems = [new_sem() for _ in range(2)]
        cv_sems = [new_sem() for _ in range(4)]
        at_sems = [new_sem() for _ in range(9)]
        io_sem = new_sem()
        out_sem = new_sem()

        hkeys_all = [("h", kc, s) for kc in range(8) for s in range(5)]
        stages = ["l0ffn1", "l0mix", "l0ffn2", "l1ffn1", "l1mix", "l1ffn2"]
        stop_i = stages.index(STOP_AFTER) if STOP_AFTER else len(stages) - 1
        for si, kind in enumerate(("H", "S0", "S1")):
            if kind == "H":
                src = xh.ap().rearrange("(kc p) t -> p kc t", p=128)
                S.op("sync", lambda e, src=src: e.dma_start(out=hT[:, :, :], in_=src), writes=hkeys_all, dma_sem=io_sem)
                subs_all = MAIN_SUBS + [EXT_SUB]
            else:
                o0 = (si - 1) * SPAN
                src = xo[:, o0:o0 + SPAN].rearrange("(kc p) t -> p kc t", p=128)
                S.op("sync", lambda e, src=src: e.dma_start(out=hT[:, :, 0:SPAN], in_=src), writes=hkeys_all, dma_sem=io_sem)
                subs_all = MAIN_SUBS
            ffn(0, 0, subs_all)
            if stop_i >= 1:
                conv_mixer(1, subs_all, kind)
            if stop_i >= 2:
                ffn(1, 2, MAIN_SUBS)
            if stop_i >= 3:
                ffn(2, 3, MAIN_SUBS)
            if stop_i >= 4:
                attn_mixer(4, kind, par_prev=(si + 1) % 2, par_cur=si % 2)
            if kind != "H":
                if stop_i >= 5:
                    ffn(3, 5, MAIN_SUBS)
                o0 = (si - 1) * SPAN
                dst = outT[:, o0:o0 + SPAN].rearrange("(kc p) t -> p kc t", p=128)
                S.op("sync", lambda e, dst=dst: e.dma_start(out=dst, in_=hT[:, :, 0:SPAN]), reads=hkeys_all, writes=[("out", si)], dma_sem=out_sem)
        S.op("sync", lambda e: e.nop(), reads=[("out", 1), ("out", 2)])
        S.finalize(sems)
        S.emit(block)
    return nc


def _bucket_onehot():
    oh = np.zeros((32, 3, 129), np.float32)
    for g, (window, dil) in enumerate(GROUPS):
        for c in range(129):
            step = 128 - c
            dist = step * dil
            if dist < 16:
                b = dist
            else:
                nf = np.float32(max(dist, 1))
                lg = int(np.float32(np.log(nf / np.float32(16)) / np.float32(math.log(2048 / 16)) * np.float32(16)))
                b = min(16 + lg, 31)
            oh[b, g, c] = 1.0
    return oh.reshape(32, 3 * 129)


_CACHE = {}


def _get_program():
    key = (STOP_AFTER, SAME_ENGINE_SYNC)
    if key not in _CACHE:
        _CACHE[key] = build_program()
    return _CACHE[key]


def prepare_inputs(x, norm_ffn1, ffn1_w_in, ffn1_w_out, norm_mix, conv_w_in, conv_w, conv_w_out,
                   attn_w_qkv, attn_q_gain, attn_k_gain, attn_w_out, rel_bias, norm_ffn2, ffn2_w_in, ffn2_w_out):
    f32 = np.float32
    x = np.asarray(x, f32)

    def gl(v):
        return np.asarray(v, f32).reshape(8, 128).T

    gains = np.concatenate([gl(norm_ffn1[0]), gl(norm_mix[0]), gl(norm_ffn2[0]),
                            gl(norm_ffn1[1]), gl(norm_mix[1]), gl(norm_ffn2[1])], axis=1)

    def win_layout(w):
        w = np.asarray(w, f32)
        gate = w[:, :DFF].reshape(8, 128, NB, 128)
        up = w[:, DFF:].reshape(8, 128, NB, 128)
        both = np.stack([gate, up], axis=3)
        return np.ascontiguousarray(both.transpose(2, 1, 0, 3, 4)).reshape(NB, 128, 2048)

    fw_in = np.stack([win_layout(ffn1_w_in[0]), win_layout(ffn2_w_in[0]), win_layout(ffn1_w_in[1]), win_layout(ffn2_w_in[1])])
    fw_out = np.stack([np.asarray(ffn1_w_out[0], f32), np.asarray(ffn2_w_out[0], f32),
                       np.asarray(ffn1_w_out[1], f32), np.asarray(ffn2_w_out[1], f32)])
    cwi = np.asarray(conv_w_in[0], f32).reshape(8, 128, 3, 8, 128)
    cw_in = np.ascontiguousarray(cwi.transpose(3, 1, 0, 2, 4)).reshape(8, 128, 3072)
    cw = np.ascontiguousarray(np.asarray(conv_w[0], f32).reshape(3, 8, 128).transpose(2, 1, 0)).reshape(128, 24)
    awq = np.asarray(attn_w_qkv[0], f32).reshape(8, 128, 3, 3, 8, 128)
    aw_qkv = np.ascontiguousarray(awq.transpose(2, 4, 1, 0, 3, 5)).reshape(3, 8, 128, 3072)
    gqk = np.zeros((128, 6), f32)
    for g in range(3):
        gqk[:, g * 2 + 0] = np.tile(np.asarray(attn_q_gain[0][g], f32), 2)
        gqk[:, g * 2 + 1] = np.tile(np.asarray(attn_k_gain[0][g], f32), 2)
    common = {
        "gains": np.ascontiguousarray(gains), "fw_in": fw_in, "fw_out": fw_out, "cw_in": cw_in, "cw": cw,
        "cw_out": np.ascontiguousarray(np.asarray(conv_w_out[0], f32)), "aw_qkv": aw_qkv,
        "aw_out": np.ascontiguousarray(np.asarray(attn_w_out[0], f32)), "gqk": gqk,
        "relb": np.ascontiguousarray(np.asarray(rel_bias, f32)), "onehot": _bucket_onehot(),
    }
    in_maps = []
    for c in range(NCORES):
        b, half = c // 2, c % 2
        xs = x[b]
        own = xs[half * 4096:(half + 1) * 4096]
        if half == 1:
            halo = np.concatenate([xs[2048:4096], xs[2046:2048]], axis=0)
            fl = 1.0
        else:
            halo = np.concatenate([xs[0:2048], xs[0:2]], axis=0)
            fl = 0.0
        m = dict(common)
        m["xh"] = np.ascontiguousarray(halo.T)
        m["xo"] = np.ascontiguousarray(own.T)
        m["flag"] = np.full((128, 1), fl, f32)
        in_maps.append(m)
    return in_maps


def kernel(**inputs):
    import time, sys
    t0 = time.time()
    in_maps = prepare_inputs(**inputs)
    t1 = time.time()
    nc = _get_program()
    t2 = time.time()
    res = run_bass_kernel_spmd(nc, in_maps, core_ids=list(range(NCORES)))
    t3 = time.time()
    print("kernel(): prep %.1fs build %.1fs run %.1fs" % (t1 - t0, t2 - t1, t3 - t2), file=sys.stderr)
    out = np.empty((4, 8192, D), np.float32)
    for c in range(NCORES):
        b, half = c // 2, c % 2
        out[b, half * 4096:(half + 1) * 4096, :] = res.results[c]["outT"].T
    return out
```

```python
import contextlib
import math
import numpy as np
import concourse.bass as bass
import concourse.mybir as mybir
from concourse.bass_utils import run_bass_kernel_spmd

F32 = mybir.dt.float32
BF16 = mybir.dt.bfloat16
AF = mybir.ActivationFunctionType
ALU = mybir.AluOpType

ENGS = ("tensor", "scalar", "vector", "gpsimd", "sync")

D = 1024
DFF = 2816
NB = DFF // 128
SPAN = 2048
EXT = 2
W = SPAN + EXT
NCORES = 8
EPS = 1e-6
GROUPS = ((128, 1), (512, 4), (2048, 16))

STOP_AFTER = None
SAME_ENGINE_SYNC = True


class Op:
    __slots__ = ("eng", "fn", "deps", "signal", "eidx", "sigval", "waits", "dma_sem", "dma_val", "gidx")

    def __init__(self, eng, fn):
        self.eng = eng
        self.fn = fn
        self.deps = []
        self.signal = False
        self.eidx = -1
        self.sigval = -1
        self.waits = []
        self.dma_sem = None
        self.dma_val = 0
        self.gidx = -1


class Sched:
    def __init__(self, nc):
        self.nc = nc
        self.ops = []
        self.eng_ops = {e: [] for e in ENGS}
        self.tiles = {}
        self.dma_cnt = {}

    def op(self, eng, fn, reads=(), writes=(), dma_sem=None):
        o = Op(eng, fn)
        o.gidx = len(self.ops)
        o.eidx = len(self.eng_ops[eng])
        if dma_sem is not None:
            k = id(dma_sem)
            self.dma_cnt[k] = self.dma_cnt.get(k, 0) + 16
            o.dma_sem = dma_sem
            o.dma_val = self.dma_cnt[k]
        deps = {}
        for k in reads:
            st = self.tiles.get(k)
            if st is not None and st[0] is not None:
                deps[st[0].gidx] = st[0]
        for k in writes:
            st = self.tiles.get(k)
            if st is not None:
                if st[0] is not None:
                    deps[st[0].gidx] = st[0]
                for r in st[1]:
                    deps[r.gidx] = r
        o.deps = [deps[g] for g in sorted(deps)]
        for k in reads:
            st = self.tiles.setdefault(k, [None, []])
            st[1].append(o)
        for k in writes:
            self.tiles[k] = [o, []]
        self.ops.append(o)
        self.eng_ops[eng].append(o)
        return o

    def barrier(self, dummy):
        keys = list(self.tiles.keys())
        self.op("vector", lambda e: e.memset(dummy, 0.0), writes=keys + ["__bar"])
        for e in ENGS:
            if e != "vector":
                self.op(e, lambda eng: eng.nop(), reads=["__bar"])

    def finalize(self, sems):
        seen_idx = {e: {x: -1 for x in ENGS} for e in ENGS}
        seen_dma = {e: {} for e in ENGS}
        for o in self.ops:
            e = o.eng
            for d in o.deps:
                if d.dma_sem is not None:
                    k = id(d.dma_sem)
                    if seen_dma[e].get(k, 0) >= d.dma_val:
                        continue
                    seen_dma[e][k] = d.dma_val
                    o.waits.append(d)
                else:
                    if d.eng == e and (e == "tensor" or not SAME_ENGINE_SYNC):
                        continue
                    if seen_idx[e][d.eng] >= d.eidx:
                        continue
                    seen_idx[e][d.eng] = d.eidx
                    d.signal = True
                    o.waits.append(d)
        for o in self.ops:
            best = {}
            for d in o.waits:
                if d.dma_sem is not None:
                    k = id(d.dma_sem)
                    if k not in best or best[k].dma_val < d.dma_val:
                        best[k] = d
            o.waits = [d for d in o.waits if d.dma_sem is None or best[id(d.dma_sem)] is d]
        for e in ENGS:
            c = 0
            for o in self.eng_ops[e]:
                if o.dma_sem is None and o.signal:
                    c += 1
                    o.sigval = c
        self.sems = sems

    def emit_engine(self, e, engobj):
        sems = self.sems
        for o in self.eng_ops[e]:
            for d in o.waits:
                if d.dma_sem is not None:
                    engobj.wait_ge(d.dma_sem, d.dma_val)
                else:
                    engobj.wait_ge(sems[d.eng], d.sigval)
            ins = o.fn(engobj)
            if o.dma_sem is not None:
                ins.then_inc(o.dma_sem, 16)
            elif o.signal:
                ins.then_inc(sems[e], 1)

    def emit(self, block):
        s = self

        @block.tensor
        def _(eng):
            s.emit_engine("tensor", eng)

        @block.scalar
        def _(eng):
            s.emit_engine("scalar", eng)

        @block.vector
        def _(eng):
            s.emit_engine("vector", eng)

        @block.gpsimd
        def _(eng):
            s.emit_engine("gpsimd", eng)

        @block.sync
        def _(eng):
            s.emit_engine("sync", eng)


def subkeys(name, idx, c0, n):
    ks = []
    c = c0
    while c < c0 + n:
        s = min(c // 512, 4)
        ks.append((name, idx, s))
        c = (c // 512 + 1) * 512
    return ks


def kx(p0, n):
    return [("kx", c) for c in range(p0 // 512, (p0 + n - 1) // 512 + 1)]


MAIN_SUBS = [(0, 512), (512, 512), (1024, 512), (1536, 512)]
EXT_SUB = (2048, 2)


def build_program():
    nc = bass.Bass("TRN2", target_bir_lowering=False)
    dt = nc.dram_tensor
    xh = dt("xh", [D, W], F32, kind="ExternalInput")
    xo = dt("xo", [D, 2 * SPAN], F32, kind="ExternalInput")
    flag_d = dt("flag", [128, 1], F32, kind="ExternalInput")
    gains_d = dt("gains", [128, 48], F32, kind="ExternalInput")
    fw_in = dt("fw_in", [4, NB, 128, 2048], F32, kind="ExternalInput")
    fw_out = dt("fw_out", [4, DFF, D], F32, kind="ExternalInput")
    cw_in = dt("cw_in", [8, 128, 3072], F32, kind="ExternalInput")
    cw_d = dt("cw", [128, 24], F32, kind="ExternalInput")
    cw_out = dt("cw_out", [D, D], F32, kind="ExternalInput")
    aw_qkv = dt("aw_qkv", [3, 8, 128, 3072], F32, kind="ExternalInput")
    aw_out = dt("aw_out", [D, D], F32, kind="ExternalInput")
    gqk_d = dt("gqk", [128, 6], F32, kind="ExternalInput")
    relb_d = dt("relb", [32, 48], F32, kind="ExternalInput")
    oneh_d = dt("onehot", [32, 3 * 129], F32, kind="ExternalInput")
    outT = dt("outT", [D, 2 * SPAN], F32, kind="ExternalOutput")
    wr_dram = dt("wr_dram", [48, 384], F32)
    kvK = dt("kvK", [2, 3, 8, 128, 2048], BF16)
    kvV = dt("kvV", [2, 3, 8, 128, 16 * 256], BF16)

    with contextlib.ExitStack() as es:
        E = es.enter_context
        hT = E(nc.sbuf_tensor("hT", [128, 8, W], F32))
        xn = E(nc.sbuf_tensor("xn", [128, 8, W], BF16))
        vext = E(nc.sbuf_tensor("vext", [128, 32, 256], BF16))
        sq = E(nc.sbuf_tensor("sq", [128, 4, 512], BF16))
        rstd = E(nc.sbuf_tensor("rstd", [128, 2, 512], F32))
        ones32 = E(nc.sbuf_tensor("ones32", [128, 128], BF16))
        bd32 = E(nc.sbuf_tensor("bd32", [128, 128], BF16))
        gains = E(nc.sbuf_tensor("gains_sb", [128, 48], F32))
        flag = E(nc.sbuf_tensor("flag_sb", [128, 1], F32))
        cwv = E(nc.sbuf_tensor("cw_sb", [128, 24], F32))
        gqk = E(nc.sbuf_tensor("gqk_sb", [128, 6], F32))
        relb = E(nc.sbuf_tensor("relb_sb", [32, 48], F32))
        vhalo = E(nc.sbuf_tensor("vhalo", [128, 8, 2], F32))
        dummy = E(nc.sbuf_tensor("dummy_sb", [128, 8], F32))
        ARENA_F32 = 18500
        arena = E(nc.sbuf_tensor("arena", [128, ARENA_F32], F32))
        ps = E(nc.psum_tensor("ps", [128, 8, 512], F32))
        sems = {e: E(nc.semaphore("s_" + e)) for e in ENGS}
        nsem = [0]

        def new_sem():
            nsem[0] += 1
            return E(nc.semaphore("d%d" % nsem[0]))

        block = E(nc.Block())
        S = Sched(nc)

        def carve(off_bytes, shape, dtype):
            esz = 2 if dtype == BF16 else 4
            n = int(np.prod(shape[1:]))
            assert off_bytes % 4 == 0
            a = arena[:, off_bytes // 4: off_bytes // 4 + (n * esz + 3) // 4]
            if dtype == BF16:
                a = a.bitcast(BF16)
            a = a[:, 0:n]
            if len(shape) > 2:
                names = "abcde"[:len(shape) - 1]
                pat = "p (" + " ".join(names) + ") -> p " + " ".join(names)
                a = a.rearrange(pat, **{names[i]: shape[1 + i] for i in range(len(shape) - 2)})
            return a, off_bytes + ((n * esz + 63) // 64) * 64

        bank_ctr = [0]

        def bank():
            b = bank_ctr[0] % 8
            bank_ctr[0] += 1
            return b

        oneh_full, _o = carve(0, [128, 3 * 129], F32)
        oneh = oneh_full[0:32, :]
        wr_full, _o = carve(_o, [128, 3, 384], F32)
        wr_sb = wr_full[0:16, :, :]
        sc = new_sem()
        S.op("sync", lambda e: e.dma_start(out=gains[:, :], in_=gains_d[:, :]), writes=["gains"], dma_sem=sc)
        S.op("sync", lambda e: e.dma_start(out=flag[:, :], in_=flag_d[:, :]), writes=["flag"], dma_sem=sc)
        S.op("sync", lambda e: e.dma_start(out=cwv[:, :], in_=cw_d[:, :]), writes=["cwv"], dma_sem=sc)
        S.op("sync", lambda e: e.dma_start(out=gqk[:, :], in_=gqk_d[:, :]), writes=["gqk"], dma_sem=sc)
        S.op("sync", lambda e: e.dma_start(out=relb[:, :], in_=relb_d[:, :]), writes=["relb"], dma_sem=sc)
        S.op("sync", lambda e: e.dma_start(out=oneh[:, :], in_=oneh_d[:, :]), writes=["oneh"], dma_sem=sc)
        S.op("vector", lambda e: e.memset(ones32[:, :], 1.0), writes=["ones32"])
        S.op("vector", lambda e: e.memset(bd32[:, :], 0.0), writes=["bd32"])
        S.op("vector", lambda e: e.memset(bd32[0:64, 0:64], 1.0), writes=["bd32"])
        S.op("vector", lambda e: e.memset(bd32[64:128, 64:128], 1.0), writes=["bd32"])
        S.op("vector", lambda e: e.memset(vext[:, :, :], 1.0), writes=["vext"])
        S.op("vector", lambda e: e.memset(wr_sb[:, :, :], 0.0), writes=["wr_sb"])
        S.op("vector", lambda e: e.memset(vhalo[:, :, :], 0.0), writes=["vhalo"])
        S.barrier(dummy[:, :])
        b0 = bank()
        for g in range(3):
            S.op("tensor", lambda e, g=g: e.matmul(ps[0:16, b0, g * 129:(g + 1) * 129], lhsT=relb[:, g * 16:(g + 1) * 16],
                                                  rhs=oneh[:, g * 129:(g + 1) * 129], start=True, stop=True),
                 reads=["relb", "oneh"], writes=[("ps", b0)])
            S.op("scalar", lambda e, g=g: e.activation(out=wr_sb[:, g, 127:256], in_=ps[0:16, b0, g * 129:(g + 1) * 129], func=AF.Exp),
                 reads=[("ps", b0)], writes=["wr_sb"])
        for g in range(3):
            S.op("sync", lambda e, g=g: e.dma_start(out=wr_dram[g * 16:(g + 1) * 16, :], in_=wr_sb[:, g, :]),
                 reads=["wr_sb"], writes=["wr_dram"], dma_sem=sc)
        S.barrier(dummy[:, :])

        def rmsnorm(gi, subs):
            for si, (c0, n) in enumerate(subs):
                ss = bank()
                for kc in range(8):
                    j = kc % 4
                    if kc % 2 == 0:
                        S.op("scalar", lambda e, kc=kc, j=j, c0=c0, n=n: e.activation(out=sq[:, j, 0:n], in_=hT[:, kc, c0:c0 + n], func=AF.Square),
                             reads=subkeys("h", kc, c0, n), writes=[("sq", j)])
                    else:
                        S.op("gpsimd", lambda e, kc=kc, j=j, c0=c0, n=n: e.tensor_tensor(out=sq[:, j, 0:n], in0=hT[:, kc, c0:c0 + n], in1=hT[:, kc, c0:c0 + n], op=ALU.mult),
                             reads=subkeys("h", kc, c0, n), writes=[("sq", j)])
                    S.op("tensor", lambda e, kc=kc, j=j, n=n, ss=ss: e.matmul(ps[:, ss, 0:n], lhsT=ones32[:, :], rhs=sq[:, j, 0:n],
                                                                             start=(kc == 0), stop=(kc == 7)),
                         reads=[("sq", j), "ones32"], writes=[("ps", ss)])
                r = si % 2
                S.op("scalar", lambda e, r=r, n=n, ss=ss: e.activation(out=rstd[:, r, 0:n], in_=ps[:, ss, 0:n], func=AF.Ln, scale=1.0 / D, bias=EPS),
                     reads=[("ps", ss)], writes=[("rstd", r)])
                S.op("scalar", lambda e, r=r, n=n: e.activation(out=rstd[:, r, 0:n], in_=rstd[:, r, 0:n], func=AF.Exp, scale=-0.5),
                     reads=[("rstd", r)], writes=[("rstd", r)])
                for kc in range(8):
                    S.op("vector", lambda e, kc=kc, r=r, c0=c0, n=n: e.scalar_tensor_tensor(
                        out=xn[:, kc, c0:c0 + n], in0=hT[:, kc, c0:c0 + n], scalar=gains[:, gi * 8 + kc:gi * 8 + kc + 1],
                        in1=rstd[:, r, 0:n], op0=ALU.mult, op1=ALU.mult),
                         reads=subkeys("h", kc, c0, n) + [("rstd", r), "gains"], writes=subkeys("xn", kc, c0, n))

        def ffn(f, gi, subs):
            rmsnorm(gi, subs)
            off = 0
            hid, off = carve(off, [128, 6, W], BF16)
            wi, off = carve(off, [128, 3, 2048], BF16)
            wo, off = carve(off, [128, 2, 6, 1024], BF16)
            sg, off = carve(off, [128, 2, 512], F32)
            assert off <= ARENA_F32 * 4, off
            groups = [(0, 6), (6, 12), (12, 17), (17, 22)]
            for gidx, (bA, bB) in enumerate(groups):
                par = gidx % 2
                for b in range(bA, bB):
                    bl = b - bA
                    sl = b % 3
                    S.op("gpsimd", lambda e, sl=sl, b=b: e.dma_start(out=wi[:, sl, :], in_=fw_in[f, b, :, :]),
                         writes=[("wi", sl)], dma_sem=ffn_wi_sems[sl])
                    S.op("gpsimd", lambda e, par=par, bl=bl, b=b: e.dma_start(out=wo[:, par, bl, :], in_=fw_out[f, b * 128:(b + 1) * 128, :]),
                         writes=[("wo", par, bl)], dma_sem=ffn_wo_sems[par][bl])
                    for (c0, n) in subs:
                        pg = bank()
                        pu = bank()

                        def mm_gate(e, sl=sl, c0=c0, n=n, pg=pg):
                            for kc in range(8):
                                ins = e.matmul(ps[:, pg, 0:n], lhsT=wi[:, sl, kc * 256:kc * 256 + 128], rhs=xn[:, kc, c0:c0 + n],
                                               start=(kc == 0), stop=(kc == 7))
                            return ins

                        def mm_up(e, sl=sl, c0=c0, n=n, pu=pu):
                            for kc in range(8):
                                ins = e.matmul(ps[:, pu, 0:n], lhsT=wi[:, sl, kc * 256 + 128:kc * 256 + 256], rhs=xn[:, kc, c0:c0 + n],
                                               start=(kc == 0), stop=(kc == 7))
                            return ins

                        xk = [k for kc in range(8) for k in subkeys("xn", kc, c0, n)]
                        S.op("tensor", mm_gate, reads=[("wi", sl)] + xk, writes=[("ps", pg)])
                        S.op("tensor", mm_up, reads=[("wi", sl)] + xk, writes=[("ps", pu)])
                        j = pg % 2
                        S.op("scalar", lambda e, j=j, n=n, pg=pg: e.activation(out=sg[:, j, 0:n], in_=ps[:, pg, 0:n], func=AF.Silu),
                             reads=[("ps", pg)], writes=[("sg", j)])
                        S.op("vector", lambda e, j=j, n=n, pu=pu, bl=bl, c0=c0: e.tensor_tensor(
                            out=hid[:, bl, c0:c0 + n], in0=sg[:, j, 0:n], in1=ps[:, pu, 0:n], op=ALU.mult),
                             reads=[("sg", j), ("ps", pu)], writes=subkeys("hid", bl, c0, n))
                nbl = bB - bA
                for d in range(8):
                    for (c0, n) in subs:
                        py = bank()

                        def mm_out(e, par=par, nbl=nbl, d=d, c0=c0, n=n, py=py):
                            for bl in range(nbl):
                                ins = e.matmul(ps[:, py, 0:n], lhsT=wo[:, par, bl, d * 128:(d + 1) * 128], rhs=hid[:, bl, c0:c0 + n],
                                               start=(bl == 0), stop=(bl == nbl - 1))
                            return ins

                        S.op("tensor", mm_out, reads=[("wo", par, bl) for bl in range(nbl)] + [k for bl in range(nbl) for k in subkeys("hid", bl, c0, n)],
                             writes=[("ps", py)])
                        S.op("vector", lambda e, d=d, c0=c0, n=n, py=py: e.scalar_tensor_tensor(
                            out=hT[:, d, c0:c0 + n], in0=ps[:, py, 0:n], scalar=0.5, in1=hT[:, d, c0:c0 + n], op0=ALU.mult, op1=ALU.add),
                             reads=[("ps", py)] + subkeys("h", d, c0, n), writes=subkeys("h", d, c0, n))
            S.barrier(dummy[:, :])

        def conv_mixer(gi, subs, span_kind):
            rmsnorm(gi, subs)
            off = 0
            wc, off = carve(off, [128, 2, 3072], BF16)
            wco, off = carve(off, [128, 2, 1024], BF16)
            c_sb, off = carve(off, [128, 2, 512], F32)
            b_sb, off = carve(off, [128, SPAN], F32)
            v_e, off = carve(off, [128, W], F32)
            acc, off = carve(off, [128, SPAN], F32)
            z, off = carve(off, [128, 2, SPAN], BF16)
            assert off <= ARENA_F32 * 4, off
            pend_out = [None]
            for fb in range(8):
                sl = fb % 2
                S.op("gpsimd", lambda e, sl=sl, fb=fb: e.dma_start(out=wc[:, sl, 0:1536], in_=cw_in[fb, :, 0:1536]),
                     writes=[("wc", sl)], dma_sem=cv_sems[sl])
                S.op("gpsimd", lambda e, sl=sl, fb=fb: e.dma_start(out=wc[:, sl, 1536:3072], in_=cw_in[fb, :, 1536:3072]),
                     writes=[("wcB", sl)], dma_sem=cvB_sems[sl])
                S.op("gpsimd", lambda e, sl=sl, fb=fb: e.dma_start(out=wco[:, sl, :], in_=cw_out[fb * 128:(fb + 1) * 128, :]),
                     writes=[("wco", sl)], dma_sem=cv_sems[2 + sl])
                if span_kind == "S0":
                    S.op("vector", lambda e, fb=fb: e.tensor_scalar(out=v_e[:, 0:2], in0=vhalo[:, fb, :], scalar1=flag[:, 0:1], scalar2=None, op0=ALU.mult),
                         reads=[("vhalo", fb), "flag"], writes=[("v_e", 5)])
                elif span_kind == "S1":
                    S.op("vector", lambda e, fb=fb: e.tensor_copy(out=v_e[:, 0:2], in_=vhalo[:, fb, :]),
                         reads=[("vhalo", fb)], writes=[("v_e", 5)])
                for (c0, n) in subs:
                    is_ext = c0 >= SPAN
                    xk = [k for kc in range(8) for k in subkeys("xn", kc, c0, n)]
                    pc = bank()
                    pu = bank()

                    def mm(e, sl=sl, c0=c0, n=n, pb=None, col=0):
                        for kc in range(8):
                            ins = e.matmul(ps[:, pb, 0:n], lhsT=wc[:, sl, kc * 384 + col:kc * 384 + col + 128], rhs=xn[:, kc, c0:c0 + n],
                                           start=(kc == 0), stop=(kc == 7))
                        return ins

                    S.op("tensor", lambda e, pc=pc, mm=mm: mm(e, pb=pc, col=128), reads=[("wc", sl), ("wcB", sl)] + xk, writes=[("ps", pc)])
                    S.op("tensor", lambda e, pu=pu, mm=mm: mm(e, pb=pu, col=256), reads=[("wc", sl), ("wcB", sl)] + xk, writes=[("ps", pu)])
                    cj = pc % 2
                    S.op("scalar", lambda e, n=n, pc=pc, cj=cj: e.activation(out=c_sb[:, cj, 0:n], in_=ps[:, pc, 0:n], func=AF.Copy),
                         reads=[("ps", pc)], writes=[("c_sb", cj)])
                    vc0 = 0 if is_ext else 2 + c0
                    vkey = ("v_e", 5) if is_ext else ("v_e", c0 // 512)
                    S.op("vector", lambda e, n=n, pu=pu, vc0=vc0, cj=cj: e.tensor_tensor(out=v_e[:, vc0:vc0 + n], in0=c_sb[:, cj, 0:n], in1=ps[:, pu, 0:n], op=ALU.mult),
                         reads=[("c_sb", cj), ("ps", pu)], writes=[vkey])
                    if not is_ext:
                        pb = bank()
                        S.op("tensor", lambda e, pb=pb, mm=mm: mm(e, pb=pb, col=0), reads=[("wc", sl), ("wcB", sl)] + xk, writes=[("ps", pb)])
                        S.op("scalar", lambda e, n=n, pb=pb, c0=c0: e.activation(out=b_sb[:, c0:c0 + n], in_=ps[:, pb, 0:n], func=AF.Copy),
                             reads=[("ps", pb)], writes=[("b_sb", c0 // 512)])
                vall = [("v_e", s) for s in range(4)] + [("v_e", 5)]
                ball = [("b_sb", s) for s in range(4)]
                S.op("vector", lambda e, fb=fb: e.tensor_scalar(out=acc[:, :], in0=v_e[:, 2:2 + SPAN], scalar1=cwv[:, fb * 3:fb * 3 + 1], scalar2=None, op0=ALU.mult),
                     reads=vall + ["cwv"], writes=["acc"])
                for lag in (1, 2):
                    S.op("vector", lambda e, fb=fb, lag=lag: e.scalar_tensor_tensor(
                        out=acc[:, :], in0=v_e[:, 2 - lag:2 - lag + SPAN], scalar=cwv[:, fb * 3 + lag:fb * 3 + lag + 1], in1=acc[:, :],
                        op0=ALU.mult, op1=ALU.add), reads=vall + ["cwv", "acc"], writes=["acc"])
                S.op("vector", lambda e, sl=sl: e.tensor_tensor(out=z[:, sl, :], in0=b_sb[:, :], in1=acc[:, :], op=ALU.mult),
                     reads=ball + ["acc"], writes=[("z", sl)])
                S.op("vector", lambda e, fb=fb: e.tensor_copy(out=vhalo[:, fb, :], in_=v_e[:, SPAN:SPAN + 2]),
                     reads=vall, writes=[("vhalo", fb)])
                def outproj(sl=sl):
                    for d in range(8):
                        for (c0, n) in MAIN_SUBS:
                            py = bank()
                            S.op("tensor", lambda e, sl=sl, d=d, c0=c0, n=n, py=py: e.matmul(
                                ps[:, py, 0:n], lhsT=wco[:, sl, d * 128:(d + 1) * 128], rhs=z[:, sl, c0:c0 + n], start=True, stop=True),
                                 reads=[("wco", sl), ("z", sl)], writes=[("ps", py)])
                            S.op("vector", lambda e, d=d, c0=c0, n=n, py=py: e.tensor_tensor(
                                out=hT[:, d, c0:c0 + n], in0=ps[:, py, 0:n], in1=hT[:, d, c0:c0 + n], op=ALU.add),
                                 reads=[("ps", py)] + subkeys("h", d, c0, n), writes=subkeys("h", d, c0, n))

                if pend_out[0] is not None:
                    pend_out[0]()
                pend_out[0] = outproj
            pend_out[0]()
            S.barrier(dummy[:, :])

        def attn_mixer(gi, span_kind, par_prev, par_cur):
            own = span_kind != "H"
            store = span_kind != "S1"
            rmsnorm(gi, MAIN_SUBS)
            off = 0
            wq, off = carve(off, [128, 2, 3072], BF16)
            wao, off = carve(off, [128, 2, 1024], BF16)
            kext, off = carve(off, [128, 2 * SPAN], BF16)
            qT, off = carve(off, [128, SPAN], BF16)
            acc, off = carve(off, [128, 2, SPAN], F32)
            rden, off = carve(off, [128, 2, 512], F32)
            ao, off = carve(off, [128, 2, SPAN], BF16)
            ebr, off = carve(off, [128, 2, 2, 128], F32)
            ebf, off = carve(off, [128, 2, 2, 2, 128], F32)
            exs, off = carve(off, [128, 2, 512], F32)
            pT, off = carve(off, [128, 2, 512], BF16)
            assert off <= ARENA_F32 * 4, off
            ctr = [0, 0, 0, 0]
            bg = []

            def bg_pop(k):
                for _ in range(min(k, len(bg))):
                    bg.pop(0)()

            def load_wq(it_):
                if it_ >= 24:
                    return
                hp_, g_, sl_ = it_ // 3, it_ % 3, it_ % 2
                S.op("gpsimd", lambda e: e.dma_start(out=wq[:, sl_, 0:1536], in_=aw_qkv[g_, hp_, :, 0:1536]),
                     writes=[("wq", sl_)], dma_sem=at_sems[sl_])
                S.op("gpsimd", lambda e: e.dma_start(out=wq[:, sl_, 1536:3072], in_=aw_qkv[g_, hp_, :, 1536:3072]),
                     writes=[("wqB", sl_)], dma_sem=atB_sems[sl_])

            for hp in range(8):
                for g in range(3):
                    P, r = GROUPS[g]
                    nbk = 16 // r
                    it = ctr[0]
                    ctr[0] += 1
                    sl = it % 2
                    if it == 0:
                        load_wq(0)
                    load_wq(it + 1)
                    if own:
                        S.op("sync", lambda e, g=g, hp=hp, P=P: e.dma_start(out=kext[:, 0:P], in_=kvK[par_prev, g, hp, :, 0:P]),
                             reads=[("kvK", par_prev, g, hp)], writes=kx(0, P), dma_sem=at_sems[2])
                        vsrc = kvV[par_prev, g, hp, :, 0:r * 256].rearrange("p (s c) -> p s c", s=r)
                        vdst = vext[:, 0:r * (nbk + 1), :].rearrange("p (s n) c -> p s n c", s=r)[:, :, 0, :]
                        S.op("sync", lambda e, vsrc=vsrc, vdst=vdst: e.dma_start(out=vdst, in_=vsrc),
                             reads=[("kvV", par_prev, g, hp)], writes=[("vext", s_ * (nbk + 1)) for s_ in range(r)], dma_sem=at_sems[3])
                        src = bass.AP(wr_dram, (g * 16 + hp * 2) * 384, [[1, 128], [384, 2], [128, 2], [1, 128]])
                        S.op("sync", lambda e, src=src: e.dma_start(out=ebr[:, :, :, :], in_=src),
                             reads=["wr_dram"], writes=["ebr"], dma_sem=at_sems[4])
                        for h in range(2):
                            for c in range(2):
                                rev = bass.AP(ebr.tensor, ebr[:, h, c, :].offset + 127, [list(ebr[:, h, c, :].ap[0]), [-1, 128]])
                                S.op("vector", lambda e, h=h, c=c, rev=rev: e.tensor_copy(out=ebf[:, 1, h, c, :], in_=rev),
                                     reads=["ebr"], writes=["ebf"])
                                if c == 0:
                                    S.op("vector", lambda e, h=h, c=c, rev=rev: e.tensor_scalar(
                                        out=ebf[:, 0, h, c, :], in0=rev, scalar1=flag[:, 0:1], scalar2=None, op0=ALU.mult),
                                         reads=["ebr", "flag"], writes=["ebf"])
                                else:
                                    S.op("vector", lambda e, h=h, c=c, rev=rev: e.tensor_copy(out=ebf[:, 0, h, c, :], in_=rev),
                                         reads=["ebr"], writes=["ebf"])
                    if own:
                        tsubs = MAIN_SUBS
                    else:
                        tsubs = [(SPAN - P, P)] if P <= 512 else MAIN_SUBS
                    state = {"pend": None, "pendb": None}

                    def head(task):
                        which, c0, n = task
                        xk = [k for kc in range(8) for k in subkeys("xn", kc, c0, n)]
                        pq = bank()

                        def mmqk(e, sl=sl, c0=c0, n=n, pq=pq, which=which):
                            for kc in range(8):
                                ins = e.matmul(ps[:, pq, 0:n], lhsT=wq[:, sl, kc * 384 + which * 128:kc * 384 + which * 128 + 128],
                                               rhs=xn[:, kc, c0:c0 + n], start=(kc == 0), stop=(kc == 7))
                            return ins

                        S.op("tensor", mmqk, reads=[("wq", sl), ("wqB", sl)] + xk, writes=[("ps", pq)])
                        j = ctr[2] % 4
                        ctr[2] += 1
                        S.op("scalar", lambda e, j=j, n=n, pq=pq: e.activation(out=sq[:, j, 0:n], in_=ps[:, pq, 0:n], func=AF.Square),
                             reads=[("ps", pq)], writes=[("sq", j)])
                        return (task, pq, j)

                    def tail(st):
                        (which, c0, n), pq, j = st
                        pss = bank()
                        S.op("tensor", lambda e, j=j, n=n, pss=pss: e.matmul(ps[:, pss, 0:n], lhsT=bd32[:, :], rhs=sq[:, j, 0:n], start=True, stop=True),
                             reads=[("sq", j), "bd32"], writes=[("ps", pss)])
                        rr = ctr[3] % 2
                        ctr[3] += 1
                        S.op("scalar", lambda e, rr=rr, n=n, pss=pss: e.activation(out=rstd[:, rr, 0:n], in_=ps[:, pss, 0:n], func=AF.Ln, scale=1.0 / 64, bias=EPS),
                             reads=[("ps", pss)], writes=[("rstd", rr)])
                        S.op("scalar", lambda e, rr=rr, n=n: e.activation(out=rstd[:, rr, 0:n], in_=rstd[:, rr, 0:n], func=AF.Exp, scale=-0.5),
                             reads=[("rstd", rr)], writes=[("rstd", rr)])
                        if which == 1:
                            dst = kext[:, P + c0:P + c0 + n]
                            dkey = kx(P + c0, n)
                        else:
                            dst = qT[:, c0:c0 + n]
                            dkey = subkeys("qT", 0, c0, n)
                        gcol = g * 2 + (1 if which == 1 else 0)
                        S.op("vector", lambda e, rr=rr, n=n, pq=pq, dst=dst, gcol=gcol: e.scalar_tensor_tensor(
                            out=dst, in0=ps[:, pq, 0:n], scalar=gqk[:, gcol:gcol + 1],
                            in1=rstd[:, rr, 0:n], op0=ALU.mult, op1=ALU.mult),
                             reads=[("ps", pq), ("rstd", rr), "gqk"], writes=dkey)

                    def flush_tail():
                        if state["pend"] is not None:
                            tail(state["pend"])
                            state["pend"] = None

                    def proj(task):
                        st = head(task)
                        flush_tail()
                        state["pend"] = st

                    def vgroup(grp):
                        pv = bank()
                        for gi_, (s, nn) in enumerate(grp):
                            m0 = (128 * (nn - 1)) * r + s
                            lo, hi = m0, m0 + 127 * r + 1

                            def mmv(e, sl=sl, m0=m0, pv=pv, gi_=gi_, r=r):
                                for kc in range(8):
                                    ins = e.matmul(ps[:, pv, gi_ * 128:(gi_ + 1) * 128], lhsT=xn[:, kc, m0:m0 + 127 * r + 1:r],
                                                   rhs=wq[:, sl, kc * 384 + 256:kc * 384 + 384], start=(kc == 0), stop=(kc == 7))
                                return ins

                            S.op("tensor", mmv, reads=[("wq", sl), ("wqB", sl)] + [k for kc in range(8) for k in subkeys("xn", kc, lo, hi - lo)], writes=[("ps", pv)])
                        flush_tail()
                        L = len(grp)
                        tixs = [s * (nbk + 1) + nn for (s, nn) in grp]
                        st_ = (tixs[1] - tixs[0]) if L > 1 else 1
                        assert all(tixs[i] == tixs[0] + i * st_ for i in range(L))
                        vsel = vext[:, tixs[0]:tixs[0] + st_ * (L - 1) + 1:st_, :]
                        psv = ps[:, pv, 0:L * 128].rearrange("p (t c) -> p t c", c=128)
                        S.op("vector", lambda e, vsel=vsel, psv=psv: e.tensor_copy(out=vsel[:, :, 0:64], in_=psv[:, :, 0:64]),
                             reads=[("ps", pv)], writes=[("vext", t) for t in tixs])
                        S.op("vector", lambda e, vsel=vsel, psv=psv: e.tensor_copy(out=vsel[:, :, 192:256], in_=psv[:, :, 64:128]),
                             reads=[("ps", pv)], writes=[("vext", t) for t in tixs])

                    def blk_head(s, nn):
                        if bank_ctr[0] % 8 == 7:
                            bank_ctr[0] += 1
                        pS0 = bank()
                        pS1 = bank()
                        assert pS1 == pS0 + 1
                        q0 = (128 * (nn - 1)) * r + s
                        e0, e1 = 128 * (nn - 1) * r, 128 * (nn + 1) * r
                        kkeys = kx(e0, e1 - e0)
                        qkeys = subkeys("qT", 0, 128 * (nn - 1) * r, 128 * r)

                        def mms(e, pS0=pS0, pS1=pS1, q0=q0, r=r, nn=nn, s=s):
                            for h in range(2):
                                for c in range(2):
                                    k0 = (128 * (nn - 1 + c)) * r + s
                                    ins = e.matmul(ps[:, (pS0, pS1)[h], c * 128:(c + 1) * 128],
                                                   lhsT=kext[h * 64:(h + 1) * 64, k0:k0 + 127 * r + 1:r],
                                                   rhs=qT[h * 64:(h + 1) * 64, q0:q0 + 127 * r + 1:r], start=True, stop=True)
                            return ins

                        S.op("tensor", mms, reads=kkeys + qkeys, writes=[("ps", pS0), ("ps", pS1)])
                        j = ctr[1] % 2
                        ctr[1] += 1
                        S.op("scalar", lambda e, j=j, pS0=pS0: e.activation(
                            out=exs[:, j, :].rearrange("p (h x) -> p h x", h=2), in_=ps[:, pS0:pS0 + 2, 0:256], func=AF.Exp, scale=0.125),
                             reads=[("ps", pS0), ("ps", pS1)], writes=[("exs", j)])
                        var = 0 if (nn == 1 and span_kind == "S0") else 1
                        meng = "vector"
                        S.op(meng, lambda e, j=j, var=var: e.tensor_tensor(
                            out=pT[:, j, :], in0=exs[:, j, :], in1=ebf[:, var, :, :, :].rearrange("p h c i -> p (h c i)"), op=ALU.mult),
                             reads=[("exs", j), "ebf"], writes=[("pT", j)])
                        return (s, nn, j, q0)

                    def blk_tail(st):
                        s, nn, j, q0 = st
                        pO = bank()

                        def mmo(e, j=j, pO=pO, nn=nn, s=s, nbk=nbk):
                            for h in range(2):
                                for c in range(2):
                                    tix = s * (nbk + 1) + (nn - 1 + c)
                                    ins = e.matmul(ps[:, pO, h * 128:(h + 1) * 128], lhsT=vext[:, tix, h * 128:(h + 1) * 128],
                                                   rhs=pT[:, j, (h * 2 + c) * 128:(h * 2 + c + 1) * 128], start=(c == 0), stop=(c == 1))
                            return ins

                        S.op("tensor", mmo, reads=[("pT", j), ("vext", s * (nbk + 1) + nn - 1), ("vext", s * (nbk + 1) + nn)], writes=[("ps", pO)])
                        adst = acc[:, :, q0:q0 + 127 * r + 1:r]
                        asrc = ps[:, pO, 0:256].rearrange("p (h i) -> p h i", h=2)
                        if g == 0:
                            S.op("scalar", lambda e, adst=adst, asrc=asrc: e.activation(out=adst, in_=asrc, func=AF.Copy),
                                 reads=[("ps", pO)], writes=["acc"])
                        else:
                            S.op("vector", lambda e, adst=adst, asrc=asrc: e.tensor_tensor(out=adst, in0=asrc, in1=adst, op=ALU.add),
                                 reads=[("ps", pO), "acc"], writes=["acc"])

                    def blocks(lst):
                        for (s, nn) in lst:
                            stb = blk_head(s, nn)
                            if state["pendb"] is not None:
                                blk_tail(state["pendb"])
                            state["pendb"] = stb

                    def flush_blocks():
                        if state["pendb"] is not None:
                            blk_tail(state["pendb"])
                            state["pendb"] = None

                    def spill():
                        if store:
                            S.op("sync", lambda e, g=g, hp=hp, P=P: e.dma_start(out=kvK[par_cur, g, hp, :, 0:P], in_=kext[:, SPAN:SPAN + P]),
                                 reads=kx(SPAN, P), writes=[("kvK", par_cur, g, hp)], dma_sem=sp_sems[it % 2])
                            vdst = kvV[par_cur, g, hp, :, 0:r * 256].rearrange("p (s c) -> p s c", s=r)
                            vsrc = vext[:, 0:r * (nbk + 1), :].rearrange("p (s n) c -> p s n c", s=r)[:, :, nbk, :]
                            S.op("sync", lambda e, vsrc=vsrc, vdst=vdst: e.dma_start(out=vdst, in_=vsrc),
                                 reads=[("vext", s_ * (nbk + 1) + nbk) for s_ in range(r)], writes=[("kvV", par_cur, g, hp)], dma_sem=sp_sems[2 + it % 2])

                    if not own:
                        for (c0, n) in tsubs:
                            proj((1, c0, n))
                        vt = [(s, nbk) for s in range(r)]
                        for ti in range(0, len(vt), 4):
                            vgroup(vt[ti:ti + 4])
                        flush_tail()
                        spill()
                        continue
                    if g < 2:
                        def punit(k):
                            c0, n = MAIN_SUBS[k]
                            proj((1, c0, n))
                            proj((0, c0, n))
                            if g == 0:
                                vgroup([(0, nn) for nn in range(4 * k + 1, 4 * k + 5)])
                            else:
                                vgroup([(s, k + 1) for s in range(4)])

                        def bunit(k):
                            if g == 0:
                                blocks([(0, nn) for nn in range(4 * k + 1, 4 * k + 5)])
                            else:
                                blocks([(s, k + 1) for s in range(4)])
                            bg_pop(3)

                        punit(0)
                        punit(1)
                        bunit(0)
                        punit(2)
                        bunit(1)
                        punit(3)
                        flush_tail()
                        bunit(2)
                        bunit(3)
                        flush_blocks()
                        spill()
                    else:
                        for (c0, n) in MAIN_SUBS:
                            proj((1, c0, n))
                            proj((0, c0, n))
                        vt = [(s, 1) for s in range(r)]
                        for ti in range(0, len(vt), 4):
                            vgroup(vt[ti:ti + 4])
                        flush_tail()
                        for ti in range(0, 16, 4):
                            blocks([(s, 1) for s in range(ti, ti + 4)])
                            bg_pop(3)
                        flush_blocks()
                        spill()
                if not own:
                    continue
                asl = hp % 2
                S.op("gpsimd", lambda e, asl=asl, hp=hp: e.dma_start(out=wao[:, asl, :], in_=aw_out[hp * 128:(hp + 1) * 128, :]),
                     writes=[("wao", asl)], dma_sem=at_sems[7 + asl])
                for q in range(4):
                    cs = slice(q * 512, (q + 1) * 512)
                    rq = q % 2
                    S.op("scalar", lambda e, cs=cs, rq=rq: e.activation(out=rden[0:64, rq, :], in_=acc[64:128, 0, cs], func=AF.Ln), reads=["acc"], writes=[("rden", rq)])
                    S.op("scalar", lambda e, cs=cs, rq=rq: e.activation(out=rden[64:128, rq, :], in_=acc[0:64, 1, cs], func=AF.Ln), reads=["acc"], writes=[("rden", rq)])
                    S.op("scalar", lambda e, rq=rq: e.activation(out=rden[:, rq, :], in_=rden[:, rq, :], func=AF.Exp, scale=-1.0), reads=[("rden", rq)], writes=[("rden", rq)])
                    S.op("vector", lambda e, asl=asl, cs=cs, rq=rq: e.tensor_tensor(out=ao[0:64, asl, cs], in0=acc[0:64, 0, cs], in1=rden[0:64, rq, :], op=ALU.mult),
                         reads=["acc", ("rden", rq)], writes=[("ao", asl, q)])
                    S.op("vector", lambda e, asl=asl, cs=cs, rq=rq: e.tensor_tensor(out=ao[64:128, asl, cs], in0=acc[64:128, 1, cs], in1=rden[64:128, rq, :], op=ALU.mult),
                         reads=["acc", ("rden", rq)], writes=[("ao", asl, q)])
                for qi, (c0, n) in enumerate(MAIN_SUBS):
                    for d in range(8):
                        def oproj(asl=asl, d=d, c0=c0, n=n, qi=qi):
                            py = bank()
                            S.op("tensor", lambda e: e.matmul(
                                ps[:, py, 0:n], lhsT=wao[:, asl, d * 128:(d + 1) * 128], rhs=ao[:, asl, c0:c0 + n], start=True, stop=True),
                                 reads=[("wao", asl), ("ao", asl, qi)], writes=[("ps", py)])
                            S.op("vector", lambda e: e.tensor_tensor(
                                out=hT[:, d, c0:c0 + n], in0=ps[:, py, 0:n], in1=hT[:, d, c0:c0 + n], op=ALU.add),
                                 reads=[("ps", py)] + subkeys("h", d, c0, n), writes=subkeys("h", d, c0, n))
                        bg.append(oproj)
            bg_pop(len(bg))
            S.barrier(dummy[:, :])

        ffn_wi_sems = [new_sem() for _ in range(3)]
        ffn_wo_sems = [[new_sem() for _ in range(6)] for _ in range(2)]
        cv_sems = [new_sem() for _ in range(4)]
        cvB_sems = [new_sem() for _ in range(2)]
        atB_sems = [new_sem() for _ in range(2)]
        sp_sems = [new_sem() for _ in range(4)]
        at_sems = [new_sem() for _ in range(9)]
        io_sem = new_sem()
        out_sem = new_sem()

        hkeys_all = [("h", kc, s) for kc in range(8) for s in range(5)]
        stages = ["l0ffn1", "l0mix", "l0ffn2", "l1ffn1", "l1mix", "l1ffn2"]
        stop_i = stages.index(STOP_AFTER) if STOP_AFTER else len(stages) - 1
        for si, kind in enumerate(("H", "S0", "S1")):
            if kind == "H":
                src = xh.ap().rearrange("(kc p) t -> p kc t", p=128)
                S.op("sync", lambda e, src=src: e.dma_start(out=hT[:, :, :], in_=src), writes=hkeys_all, dma_sem=io_sem)
                subs_all = MAIN_SUBS + [EXT_SUB]
            else:
                o0 = (si - 1) * SPAN
                src = xo[:, o0:o0 + SPAN].rearrange("(kc p) t -> p kc t", p=128)
                S.op("sync", lambda e, src=src: e.dma_start(out=hT[:, :, 0:SPAN], in_=src), writes=hkeys_all, dma_sem=io_sem)
                subs_all = MAIN_SUBS
            ffn(0, 0, subs_all)
            if stop_i >= 1:
                conv_mixer(1, subs_all, kind)
            if stop_i >= 2:
                ffn(1, 2, MAIN_SUBS)
            if stop_i >= 3:
                ffn(2, 3, MAIN_SUBS)
            if stop_i >= 4:
                attn_mixer(4, kind, par_prev=(si + 1) % 2, par_cur=si % 2)
            if kind != "H":
                if stop_i >= 5:
                    ffn(3, 5, MAIN_SUBS)
                o0 = (si - 1) * SPAN
                dst = outT[:, o0:o0 + SPAN].rearrange("(kc p) t -> p kc t", p=128)
                S.op("sync", lambda e, dst=dst: e.dma_start(out=dst, in_=hT[:, :, 0:SPAN]), reads=hkeys_all, writes=[("out", si)], dma_sem=out_sem)
        S.op("sync", lambda e: e.nop(), reads=[("out", 1), ("out", 2)])
        S.finalize(sems)
        S.emit(block)
    return nc


def _bucket_onehot():
    oh = np.zeros((32, 3, 129), np.float32)
    for g, (window, dil) in enumerate(GROUPS):
        for c in range(129):
            step = 128 - c
            dist = step * dil
            if dist < 16:
                b = dist
            else:
                nf = np.float32(max(dist, 1))
                lg = int(np.float32(np.log(nf / np.float32(16)) / np.float32(math.log(2048 / 16)) * np.float32(16)))
                b = min(16 + lg, 31)
            oh[b, g, c] = 1.0
    return oh.reshape(32, 3 * 129)


_CACHE = {}


def _get_program():
    key = (STOP_AFTER, SAME_ENGINE_SYNC)
    if key not in _CACHE:
        _CACHE[key] = build_program()
    return _CACHE[key]


def prepare_inputs(x, norm_ffn1, ffn1_w_in, ffn1_w_out, norm_mix, conv_w_in, conv_w, conv_w_out,
                   attn_w_qkv, attn_q_gain, attn_k_gain, attn_w_out, rel_bias, norm_ffn2, ffn2_w_in, ffn2_w_out):
    f32 = np.float32
    x = np.asarray(x, f32)

    def gl(v):
        return np.asarray(v, f32).reshape(8, 128).T

    gains = np.concatenate([gl(norm_ffn1[0]), gl(norm_mix[0]), gl(norm_ffn2[0]),
                            gl(norm_ffn1[1]), gl(norm_mix[1]), gl(norm_ffn2[1])], axis=1)

    def win_layout(w):
        w = np.asarray(w, f32)
        gate = w[:, :DFF].reshape(8, 128, NB, 128)
        up = w[:, DFF:].reshape(8, 128, NB, 128)
        both = np.stack([gate, up], axis=3)
        return np.ascontiguousarray(both.transpose(2, 1, 0, 3, 4)).reshape(NB, 128, 2048)

    fw_in = np.stack([win_layout(ffn1_w_in[0]), win_layout(ffn2_w_in[0]), win_layout(ffn1_w_in[1]), win_layout(ffn2_w_in[1])])
    fw_out = np.stack([np.asarray(ffn1_w_out[0], f32), np.asarray(ffn2_w_out[0], f32),
                       np.asarray(ffn1_w_out[1], f32), np.asarray(ffn2_w_out[1], f32)])
    cwi = np.asarray(conv_w_in[0], f32).reshape(8, 128, 3, 8, 128)
    cw_in = np.ascontiguousarray(cwi.transpose(3, 1, 0, 2, 4)).reshape(8, 128, 3072)
    cw = np.ascontiguousarray(np.asarray(conv_w[0], f32).reshape(3, 8, 128).transpose(2, 1, 0)).reshape(128, 24)
    awq = np.asarray(attn_w_qkv[0], f32).reshape(8, 128, 3, 3, 8, 128)
    aw_qkv = np.ascontiguousarray(awq.transpose(2, 4, 1, 0, 3, 5)).reshape(3, 8, 128, 3072)
    gqk = np.zeros((128, 6), f32)
    for g in range(3):
        gqk[:, g * 2 + 0] = np.tile(np.asarray(attn_q_gain[0][g], f32), 2)
        gqk[:, g * 2 + 1] = np.tile(np.asarray(attn_k_gain[0][g], f32), 2)
    common = {
        "gains": np.ascontiguousarray(gains), "fw_in": fw_in, "fw_out": fw_out, "cw_in": cw_in, "cw": cw,
        "cw_out": np.ascontiguousarray(np.asarray(conv_w_out[0], f32)), "aw_qkv": aw_qkv,
        "aw_out": np.ascontiguousarray(np.asarray(attn_w_out[0], f32)), "gqk": gqk,
        "relb": np.ascontiguousarray(np.asarray(rel_bias, f32)), "onehot": _bucket_onehot(),
    }
    in_maps = []
    for c in range(NCORES):
        b, half = c // 2, c % 2
        xs = x[b]
        own = xs[half * 4096:(half + 1) * 4096]
        if half == 1:
            halo = np.concatenate([xs[2048:4096], xs[2046:2048]], axis=0)
            fl = 1.0
        else:
            halo = np.concatenate([xs[0:2048], xs[0:2]], axis=0)
            fl = 0.0
        m = dict(common)
        m["xh"] = np.ascontiguousarray(halo.T)
        m["xo"] = np.ascontiguousarray(own.T)
        m["flag"] = np.full((128, 1), fl, f32)
        in_maps.append(m)
    return in_maps


def kernel(**inputs):
    import time, sys
    t0 = time.time()
    in_maps = prepare_inputs(**inputs)
    t1 = time.time()
    nc = _get_program()
    t2 = time.time()
    res = run_bass_kernel_spmd(nc, in_maps, core_ids=list(range(NCORES)))
    t3 = time.time()
    print("kernel(): prep %.1fs build %.1fs run %.1fs" % (t1 - t0, t2 - t1, t3 - t2), file=sys.stderr)
    out = np.empty((4, 8192, D), np.float32)
    for c in range(NCORES):
        b, half = c // 2, c % 2
        out[b, half * 4096:(half + 1) * 4096, :] = res.results[c]["outT"].T
    return out
```

```python
import contextlib
import math
import numpy as np
import concourse.bass as bass
import concourse.mybir as mybir
from concourse.bass_utils import run_bass_kernel_spmd

F32 = mybir.dt.float32
BF16 = mybir.dt.bfloat16
AF = mybir.ActivationFunctionType
ALU = mybir.AluOpType

ENGS = ("tensor", "scalar", "vector", "gpsimd", "sync")

D = 1024
DFF = 2816
NB = DFF // 128
SPAN = 2048
EXT = 2
W = SPAN + EXT
NCORES = 8
EPS = 1e-6
GROUPS = ((128, 1), (512, 4), (2048, 16))

STOP_AFTER = None
SAME_ENGINE_SYNC = True


class Op:
    __slots__ = ("eng", "fn", "deps", "signal", "eidx", "sigval", "waits", "dma_sem", "dma_val", "gidx")

    def __init__(self, eng, fn):
        self.eng = eng
        self.fn = fn
        self.deps = []
        self.signal = False
        self.eidx = -1
        self.sigval = -1
        self.waits = []
        self.dma_sem = None
        self.dma_val = 0
        self.gidx = -1


class Sched:
    def __init__(self, nc):
        self.nc = nc
        self.ops = []
        self.eng_ops = {e: [] for e in ENGS}
        self.tiles = {}
        self.dma_cnt = {}

    def op(self, eng, fn, reads=(), writes=(), dma_sem=None):
        o = Op(eng, fn)
        o.gidx = len(self.ops)
        o.eidx = len(self.eng_ops[eng])
        if dma_sem is not None:
            k = id(dma_sem)
            self.dma_cnt[k] = self.dma_cnt.get(k, 0) + 16
            o.dma_sem = dma_sem
            o.dma_val = self.dma_cnt[k]
        deps = {}
        for k in reads:
            st = self.tiles.get(k)
            if st is not None and st[0] is not None:
                deps[st[0].gidx] = st[0]
        for k in writes:
            st = self.tiles.get(k)
            if st is not None:
                if st[0] is not None:
                    deps[st[0].gidx] = st[0]
                for r in st[1]:
                    deps[r.gidx] = r
        o.deps = [deps[g] for g in sorted(deps)]
        for k in reads:
            st = self.tiles.setdefault(k, [None, []])
            st[1].append(o)
        for k in writes:
            self.tiles[k] = [o, []]
        self.ops.append(o)
        self.eng_ops[eng].append(o)
        return o

    def barrier(self, dummy):
        keys = list(self.tiles.keys())
        self.op("vector", lambda e: e.memset(dummy, 0.0), writes=keys + ["__bar"])
        for e in ENGS:
            if e != "vector":
                self.op(e, lambda eng: eng.nop(), reads=["__bar"])

    def finalize(self, sems):
        seen_idx = {e: {x: -1 for x in ENGS} for e in ENGS}
        seen_dma = {e: {} for e in ENGS}
        for o in self.ops:
            e = o.eng
            for d in o.deps:
                if d.dma_sem is not None:
                    k = id(d.dma_sem)
                    if seen_dma[e].get(k, 0) >= d.dma_val:
                        continue
                    seen_dma[e][k] = d.dma_val
                    o.waits.append(d)
                else:
                    if d.eng == e and (e == "tensor" or not SAME_ENGINE_SYNC):
                        continue
                    if seen_idx[e][d.eng] >= d.eidx:
                        continue
                    seen_idx[e][d.eng] = d.eidx
                    d.signal = True
                    o.waits.append(d)
        for o in self.ops:
            best = {}
            for d in o.waits:
                if d.dma_sem is not None:
                    k = id(d.dma_sem)
                    if k not in best or best[k].dma_val < d.dma_val:
                        best[k] = d
            o.waits = [d for d in o.waits if d.dma_sem is None or best[id(d.dma_sem)] is d]
        for e in ENGS:
            c = 0
            for o in self.eng_ops[e]:
                if o.dma_sem is None and o.signal:
                    c += 1
                    o.sigval = c
        self.sems = sems

    def emit_engine(self, e, engobj):
        sems = self.sems
        for o in self.eng_ops[e]:
            for d in o.waits:
                if d.dma_sem is not None:
                    engobj.wait_ge(d.dma_sem, d.dma_val)
                else:
                    engobj.wait_ge(sems[d.eng], d.sigval)
            ins = o.fn(engobj)
            if o.dma_sem is not None:
                ins.then_inc(o.dma_sem, 16)
            elif o.signal:
                ins.then_inc(sems[e], 1)

    def emit(self, block):
        s = self

        @block.tensor
        def _(eng):
            s.emit_engine("tensor", eng)

        @block.scalar
        def _(eng):
            s.emit_engine("scalar", eng)

        @block.vector
        def _(eng):
            s.emit_engine("vector", eng)

        @block.gpsimd
        def _(eng):
            s.emit_engine("gpsimd", eng)

        @block.sync
        def _(eng):
            s.emit_engine("sync", eng)


def subkeys(name, idx, c0, n):
    ks = []
    c = c0
    while c < c0 + n:
        s = min(c // 512, 4)
        ks.append((name, idx, s))
        c = (c // 512 + 1) * 512
    return ks


def kx(p0, n):
    return [("kx", c) for c in range(p0 // 512, (p0 + n - 1) // 512 + 1)]


MAIN_SUBS = [(0, 512), (512, 512), (1024, 512), (1536, 512)]
EXT_SUB = (2048, 2)


def build_program():
    nc = bass.Bass("TRN2", target_bir_lowering=False)
    dt = nc.dram_tensor
    xh = dt("xh", [D, W], F32, kind="ExternalInput")
    xo = dt("xo", [D, 2 * SPAN], F32, kind="ExternalInput")
    flag_d = dt("flag", [128, 1], F32, kind="ExternalInput")
    gains_d = dt("gains", [128, 48], F32, kind="ExternalInput")
    fw_in = dt("fw_in", [4, NB, 128, 2048], F32, kind="ExternalInput")
    fw_out = dt("fw_out", [4, DFF, D], F32, kind="ExternalInput")
    cw_in = dt("cw_in", [8, 128, 3072], F32, kind="ExternalInput")
    cw_d = dt("cw", [128, 24], F32, kind="ExternalInput")
    cw_out = dt("cw_out", [D, D], F32, kind="ExternalInput")
    aw_qkv = dt("aw_qkv", [3, 8, 128, 3072], F32, kind="ExternalInput")
    aw_out = dt("aw_out", [D, D], F32, kind="ExternalInput")
    gqk_d = dt("gqk", [128, 6], F32, kind="ExternalInput")
    relb_d = dt("relb", [32, 48], F32, kind="ExternalInput")
    oneh_d = dt("onehot", [32, 3 * 129], F32, kind="ExternalInput")
    outT = dt("outT", [D, 2 * SPAN], F32, kind="ExternalOutput")
    wr_dram = dt("wr_dram", [48, 384], F32)
    kvK = dt("kvK", [2, 3, 8, 128, 2048], BF16)
    kvV = dt("kvV", [2, 3, 8, 128, 16 * 256], BF16)

    with contextlib.ExitStack() as es:
        E = es.enter_context
        hT = E(nc.sbuf_tensor("hT", [128, 8, W], F32))
        xn = E(nc.sbuf_tensor("xn", [128, 8, W], BF16))
        vext = E(nc.sbuf_tensor("vext", [128, 32, 256], BF16))
        sq = E(nc.sbuf_tensor("sq", [128, 4, 512], BF16))
        rstd = E(nc.sbuf_tensor("rstd", [128, 2, 512], F32))
        ones32 = E(nc.sbuf_tensor("ones32", [128, 128], BF16))
        bd32 = E(nc.sbuf_tensor("bd32", [128, 128], BF16))
        gains = E(nc.sbuf_tensor("gains_sb", [128, 48], F32))
        flag = E(nc.sbuf_tensor("flag_sb", [128, 1], F32))
        cwv = E(nc.sbuf_tensor("cw_sb", [128, 24], F32))
        gqk = E(nc.sbuf_tensor("gqk_sb", [128, 6], F32))
        relb = E(nc.sbuf_tensor("relb_sb", [32, 48], F32))
        vhalo = E(nc.sbuf_tensor("vhalo", [128, 8, 2], F32))
        dummy = E(nc.sbuf_tensor("dummy_sb", [128, 8], F32))
        ARENA_F32 = 18500
        arena = E(nc.sbuf_tensor("arena", [128, ARENA_F32], F32))
        ps = E(nc.psum_tensor("ps", [128, 8, 512], F32))
        sems = {e: E(nc.semaphore("s_" + e)) for e in ENGS}
        nsem = [0]

        def new_sem():
            nsem[0] += 1
            return E(nc.semaphore("d%d" % nsem[0]))

        block = E(nc.Block())
        S = Sched(nc)

        def carve(off_bytes, shape, dtype):
            esz = 2 if dtype == BF16 else 4
            n = int(np.prod(shape[1:]))
            assert off_bytes % 4 == 0
            a = arena[:, off_bytes // 4: off_bytes // 4 + (n * esz + 3) // 4]
            if dtype == BF16:
                a = a.bitcast(BF16)
            a = a[:, 0:n]
            if len(shape) > 2:
                names = "abcde"[:len(shape) - 1]
                pat = "p (" + " ".join(names) + ") -> p " + " ".join(names)
                a = a.rearrange(pat, **{names[i]: shape[1 + i] for i in range(len(shape) - 2)})
            return a, off_bytes + ((n * esz + 63) // 64) * 64

        bank_ctr = [0]

        def bank():
            b = bank_ctr[0] % 8
            bank_ctr[0] += 1
            return b

        oneh_full, _o = carve(0, [128, 3 * 129], F32)
        oneh = oneh_full[0:32, :]
        wr_full, _o = carve(_o, [128, 3, 384], F32)
        wr_sb = wr_full[0:16, :, :]
        sc = new_sem()
        S.op("sync", lambda e: e.dma_start(out=gains[:, :], in_=gains_d[:, :]), writes=["gains"], dma_sem=sc)
        S.op("sync", lambda e: e.dma_start(out=flag[:, :], in_=flag_d[:, :]), writes=["flag"], dma_sem=sc)
        S.op("sync", lambda e: e.dma_start(out=cwv[:, :], in_=cw_d[:, :]), writes=["cwv"], dma_sem=sc)
        S.op("sync", lambda e: e.dma_start(out=gqk[:, :], in_=gqk_d[:, :]), writes=["gqk"], dma_sem=sc)
        S.op("sync", lambda e: e.dma_start(out=relb[:, :], in_=relb_d[:, :]), writes=["relb"], dma_sem=sc)
        S.op("sync", lambda e: e.dma_start(out=oneh[:, :], in_=oneh_d[:, :]), writes=["oneh"], dma_sem=sc)
        S.op("vector", lambda e: e.memset(ones32[:, :], 1.0), writes=["ones32"])
        S.op("vector", lambda e: e.memset(bd32[:, :], 0.0), writes=["bd32"])
        S.op("vector", lambda e: e.memset(bd32[0:64, 0:64], 1.0), writes=["bd32"])
        S.op("vector", lambda e: e.memset(bd32[64:128, 64:128], 1.0), writes=["bd32"])
        S.op("vector", lambda e: e.memset(vext[:, :, :], 1.0), writes=["vext"])
        S.op("vector", lambda e: e.memset(wr_sb[:, :, :], 0.0), writes=["wr_sb"])
        S.op("vector", lambda e: e.memset(vhalo[:, :, :], 0.0), writes=["vhalo"])
        S.barrier(dummy[:, :])
        b0 = bank()
        for g in range(3):
            S.op("tensor", lambda e, g=g: e.matmul(ps[0:16, b0, g * 129:(g + 1) * 129], lhsT=relb[:, g * 16:(g + 1) * 16],
                                                  rhs=oneh[:, g * 129:(g + 1) * 129], start=True, stop=True),
                 reads=["relb", "oneh"], writes=[("ps", b0)])
            S.op("scalar", lambda e, g=g: e.activation(out=wr_sb[:, g, 127:256], in_=ps[0:16, b0, g * 129:(g + 1) * 129], func=AF.Exp),
                 reads=[("ps", b0)], writes=["wr_sb"])
        for g in range(3):
            S.op("sync", lambda e, g=g: e.dma_start(out=wr_dram[g * 16:(g + 1) * 16, :], in_=wr_sb[:, g, :]),
                 reads=["wr_sb"], writes=["wr_dram"], dma_sem=sc)
        S.barrier(dummy[:, :])

        def rmsnorm(gi, subs):
            for si, (c0, n) in enumerate(subs):
                ss = bank()
                for kc in range(8):
                    j = kc % 4
                    if kc % 2 == 0:
                        S.op("scalar", lambda e, kc=kc, j=j, c0=c0, n=n: e.activation(out=sq[:, j, 0:n], in_=hT[:, kc, c0:c0 + n], func=AF.Square),
                             reads=subkeys("h", kc, c0, n), writes=[("sq", j)])
                    else:
                        S.op("gpsimd", lambda e, kc=kc, j=j, c0=c0, n=n: e.tensor_tensor(out=sq[:, j, 0:n], in0=hT[:, kc, c0:c0 + n], in1=hT[:, kc, c0:c0 + n], op=ALU.mult),
                             reads=subkeys("h", kc, c0, n), writes=[("sq", j)])
                    S.op("tensor", lambda e, kc=kc, j=j, n=n, ss=ss: e.matmul(ps[:, ss, 0:n], lhsT=ones32[:, :], rhs=sq[:, j, 0:n],
                                                                             start=(kc == 0), stop=(kc == 7)),
                         reads=[("sq", j), "ones32"], writes=[("ps", ss)])
                r = si % 2
                S.op("scalar", lambda e, r=r, n=n, ss=ss: e.activation(out=rstd[:, r, 0:n], in_=ps[:, ss, 0:n], func=AF.Ln, scale=1.0 / D, bias=EPS),
                     reads=[("ps", ss)], writes=[("rstd", r)])
                S.op("scalar", lambda e, r=r, n=n: e.activation(out=rstd[:, r, 0:n], in_=rstd[:, r, 0:n], func=AF.Exp, scale=-0.5),
                     reads=[("rstd", r)], writes=[("rstd", r)])
                for kc in range(8):
                    S.op("vector", lambda e, kc=kc, r=r, c0=c0, n=n: e.scalar_tensor_tensor(
                        out=xn[:, kc, c0:c0 + n], in0=hT[:, kc, c0:c0 + n], scalar=gains[:, gi * 8 + kc:gi * 8 + kc + 1],
                        in1=rstd[:, r, 0:n], op0=ALU.mult, op1=ALU.mult),
                         reads=subkeys("h", kc, c0, n) + [("rstd", r), "gains"], writes=subkeys("xn", kc, c0, n))

        def ffn(f, gi, subs):
            rmsnorm(gi, subs)
            off = 0
            hid, off = carve(off, [128, 6, W], BF16)
            wi, off = carve(off, [128, 3, 2048], BF16)
            wo, off = carve(off, [128, 2, 6, 1024], BF16)
            sg, off = carve(off, [128, 2, 512], F32)
            assert off <= ARENA_F32 * 4, off
            groups = [(0, 6), (6, 12), (12, 17), (17, 22)]
            for gidx, (bA, bB) in enumerate(groups):
                par = gidx % 2
                for b in range(bA, bB):
                    bl = b - bA
                    sl = b % 3
                    S.op("gpsimd", lambda e, sl=sl, b=b: e.dma_start(out=wi[:, sl, :], in_=fw_in[f, b, :, :]),
                         writes=[("wi", sl)], dma_sem=ffn_wi_sems[sl])
                    S.op("gpsimd", lambda e, par=par, bl=bl, b=b: e.dma_start(out=wo[:, par, bl, :], in_=fw_out[f, b * 128:(b + 1) * 128, :]),
                         writes=[("wo", par, bl)], dma_sem=ffn_wo_sems[par][bl])
                    for (c0, n) in subs:
                        pg = bank()
                        pu = bank()

                        def mm_gate(e, sl=sl, c0=c0, n=n, pg=pg):
                            for kc in range(8):
                                ins = e.matmul(ps[:, pg, 0:n], lhsT=wi[:, sl, kc * 256:kc * 256 + 128], rhs=xn[:, kc, c0:c0 + n],
                                               start=(kc == 0), stop=(kc == 7))
                            return ins

                        def mm_up(e, sl=sl, c0=c0, n=n, pu=pu):
                            for kc in range(8):
                                ins = e.matmul(ps[:, pu, 0:n], lhsT=wi[:, sl, kc * 256 + 128:kc * 256 + 256], rhs=xn[:, kc, c0:c0 + n],
                                               start=(kc == 0), stop=(kc == 7))
                            return ins

                        xk = [k for kc in range(8) for k in subkeys("xn", kc, c0, n)]
                        S.op("tensor", mm_gate, reads=[("wi", sl)] + xk, writes=[("ps", pg)])
                        S.op("tensor", mm_up, reads=[("wi", sl)] + xk, writes=[("ps", pu)])
                        j = pg % 2
                        S.op("scalar", lambda e, j=j, n=n, pg=pg: e.activation(out=sg[:, j, 0:n], in_=ps[:, pg, 0:n], func=AF.Silu),
                             reads=[("ps", pg)], writes=[("sg", j)])
                        S.op("vector", lambda e, j=j, n=n, pu=pu, bl=bl, c0=c0: e.tensor_tensor(
                            out=hid[:, bl, c0:c0 + n], in0=sg[:, j, 0:n], in1=ps[:, pu, 0:n], op=ALU.mult),
                             reads=[("sg", j), ("ps", pu)], writes=subkeys("hid", bl, c0, n))
                nbl = bB - bA
                for d in range(8):
                    for (c0, n) in subs:
                        py = bank()

                        def mm_out(e, par=par, nbl=nbl, d=d, c0=c0, n=n, py=py):
                            for bl in range(nbl):
                                ins = e.matmul(ps[:, py, 0:n], lhsT=wo[:, par, bl, d * 128:(d + 1) * 128], rhs=hid[:, bl, c0:c0 + n],
                                               start=(bl == 0), stop=(bl == nbl - 1))
                            return ins

                        S.op("tensor", mm_out, reads=[("wo", par, bl) for bl in range(nbl)] + [k for bl in range(nbl) for k in subkeys("hid", bl, c0, n)],
                             writes=[("ps", py)])
                        S.op("vector", lambda e, d=d, c0=c0, n=n, py=py: e.scalar_tensor_tensor(
                            out=hT[:, d, c0:c0 + n], in0=ps[:, py, 0:n], scalar=0.5, in1=hT[:, d, c0:c0 + n], op0=ALU.mult, op1=ALU.add),
                             reads=[("ps", py)] + subkeys("h", d, c0, n), writes=subkeys("h", d, c0, n))
            S.barrier(dummy[:, :])

        def conv_mixer(gi, subs, span_kind):
            rmsnorm(gi, subs)
            off = 0
            wc, off = carve(off, [128, 2, 3072], BF16)
            wco, off = carve(off, [128, 2, 1024], BF16)
            c_sb, off = carve(off, [128, 2, 512], F32)
            b_sb, off = carve(off, [128, SPAN], F32)
            v_e, off = carve(off, [128, W], F32)
            acc, off = carve(off, [128, SPAN], F32)
            z, off = carve(off, [128, 2, SPAN], BF16)
            assert off <= ARENA_F32 * 4, off
            pend_out = [None]
            for fb in range(8):
                sl = fb % 2
                S.op("gpsimd", lambda e, sl=sl, fb=fb: e.dma_start(out=wc[:, sl, 0:1536], in_=cw_in[fb, :, 0:1536]),
                     writes=[("wc", sl)], dma_sem=cv_sems[sl])
                S.op("gpsimd", lambda e, sl=sl, fb=fb: e.dma_start(out=wc[:, sl, 1536:3072], in_=cw_in[fb, :, 1536:3072]),
                     writes=[("wcB", sl)], dma_sem=cvB_sems[sl])
                S.op("gpsimd", lambda e, sl=sl, fb=fb: e.dma_start(out=wco[:, sl, :], in_=cw_out[fb * 128:(fb + 1) * 128, :]),
                     writes=[("wco", sl)], dma_sem=cv_sems[2 + sl])
                if span_kind == "S0":
                    S.op("vector", lambda e, fb=fb: e.tensor_scalar(out=v_e[:, 0:2], in0=vhalo[:, fb, :], scalar1=flag[:, 0:1], scalar2=None, op0=ALU.mult),
                         reads=[("vhalo", fb), "flag"], writes=[("v_e", 5)])
                elif span_kind == "S1":
                    S.op("vector", lambda e, fb=fb: e.tensor_copy(out=v_e[:, 0:2], in_=vhalo[:, fb, :]),
                         reads=[("vhalo", fb)], writes=[("v_e", 5)])
                for (c0, n) in subs:
                    is_ext = c0 >= SPAN
                    xk = [k for kc in range(8) for k in subkeys("xn", kc, c0, n)]
                    pc = bank()
                    pu = bank()

                    def mm(e, sl=sl, c0=c0, n=n, pb=None, col=0):
                        for kc in range(8):
                            ins = e.matmul(ps[:, pb, 0:n], lhsT=wc[:, sl, kc * 384 + col:kc * 384 + col + 128], rhs=xn[:, kc, c0:c0 + n],
                                           start=(kc == 0), stop=(kc == 7))
                        return ins

                    S.op("tensor", lambda e, pc=pc, mm=mm: mm(e, pb=pc, col=128), reads=[("wc", sl), ("wcB", sl)] + xk, writes=[("ps", pc)])
                    S.op("tensor", lambda e, pu=pu, mm=mm: mm(e, pb=pu, col=256), reads=[("wc", sl), ("wcB", sl)] + xk, writes=[("ps", pu)])
                    cj = pc % 2
                    S.op("scalar", lambda e, n=n, pc=pc, cj=cj: e.activation(out=c_sb[:, cj, 0:n], in_=ps[:, pc, 0:n], func=AF.Copy),
                         reads=[("ps", pc)], writes=[("c_sb", cj)])
                    vc0 = 0 if is_ext else 2 + c0
                    vkey = ("v_e", 5) if is_ext else ("v_e", c0 // 512)
                    S.op("vector", lambda e, n=n, pu=pu, vc0=vc0, cj=cj: e.tensor_tensor(out=v_e[:, vc0:vc0 + n], in0=c_sb[:, cj, 0:n], in1=ps[:, pu, 0:n], op=ALU.mult),
                         reads=[("c_sb", cj), ("ps", pu)], writes=[vkey])
                    if not is_ext:
                        pb = bank()
                        S.op("tensor", lambda e, pb=pb, mm=mm: mm(e, pb=pb, col=0), reads=[("wc", sl), ("wcB", sl)] + xk, writes=[("ps", pb)])
                        S.op("scalar", lambda e, n=n, pb=pb, c0=c0: e.activation(out=b_sb[:, c0:c0 + n], in_=ps[:, pb, 0:n], func=AF.Copy),
                             reads=[("ps", pb)], writes=[("b_sb", c0 // 512)])
                vall = [("v_e", s) for s in range(4)] + [("v_e", 5)]
                ball = [("b_sb", s) for s in range(4)]
                S.op("vector", lambda e, fb=fb: e.tensor_scalar(out=acc[:, :], in0=v_e[:, 2:2 + SPAN], scalar1=cwv[:, fb * 3:fb * 3 + 1], scalar2=None, op0=ALU.mult),
                     reads=vall + ["cwv"], writes=["acc"])
                for lag in (1, 2):
                    S.op("vector", lambda e, fb=fb, lag=lag: e.scalar_tensor_tensor(
                        out=acc[:, :], in0=v_e[:, 2 - lag:2 - lag + SPAN], scalar=cwv[:, fb * 3 + lag:fb * 3 + lag + 1], in1=acc[:, :],
                        op0=ALU.mult, op1=ALU.add), reads=vall + ["cwv", "acc"], writes=["acc"])
                S.op("vector", lambda e, sl=sl: e.tensor_tensor(out=z[:, sl, :], in0=b_sb[:, :], in1=acc[:, :], op=ALU.mult),
                     reads=ball + ["acc"], writes=[("z", sl)])
                S.op("vector", lambda e, fb=fb: e.tensor_copy(out=vhalo[:, fb, :], in_=v_e[:, SPAN:SPAN + 2]),
                     reads=vall, writes=[("vhalo", fb)])
                def outproj(sl=sl):
                    for d in range(8):
                        for (c0, n) in MAIN_SUBS:
                            py = bank()
                            S.op("tensor", lambda e, sl=sl, d=d, c0=c0, n=n, py=py: e.matmul(
                                ps[:, py, 0:n], lhsT=wco[:, sl, d * 128:(d + 1) * 128], rhs=z[:, sl, c0:c0 + n], start=True, stop=True),
                                 reads=[("wco", sl), ("z", sl)], writes=[("ps", py)])
                            S.op("vector", lambda e, d=d, c0=c0, n=n, py=py: e.tensor_tensor(
                                out=hT[:, d, c0:c0 + n], in0=ps[:, py, 0:n], in1=hT[:, d, c0:c0 + n], op=ALU.add),
                                 reads=[("ps", py)] + subkeys("h", d, c0, n), writes=subkeys("h", d, c0, n))

                if pend_out[0] is not None:
                    pend_out[0]()
                pend_out[0] = outproj
            pend_out[0]()
            S.barrier(dummy[:, :])

        def attn_mixer(gi, span_kind, par_prev, par_cur):
            own = span_kind != "H"
            store = span_kind != "S1"
            rmsnorm(gi, MAIN_SUBS)
            off = 0
            wq, off = carve(off, [128, 2, 3072], BF16)
            wao, off = carve(off, [128, 2, 1024], BF16)
            kext, off = carve(off, [128, 2 * SPAN], BF16)
            qT, off = carve(off, [128, SPAN], BF16)
            acc, off = carve(off, [128, 2, SPAN], F32)
            rden, off = carve(off, [128, 2, 512], F32)
            ao, off = carve(off, [128, 2, SPAN], BF16)
            ebr, off = carve(off, [128, 2, 2, 128], F32)
            ebf, off = carve(off, [128, 2, 2, 2, 128], F32)
            exs, off = carve(off, [128, 3, 512], F32)
            pT, off = carve(off, [128, 3, 512], BF16)
            assert off <= ARENA_F32 * 4, off
            ctr = [0, 0, 0, 0]
            bg = []

            def bg_pop(k):
                for _ in range(min(k, len(bg))):
                    bg.pop(0)()

            def load_wq(it_):
                if it_ >= 24:
                    return
                hp_, g_, sl_ = it_ // 3, it_ % 3, it_ % 2
                S.op("gpsimd", lambda e: e.dma_start(out=wq[:, sl_, 0:1536], in_=aw_qkv[g_, hp_, :, 0:1536]),
                     writes=[("wq", sl_)], dma_sem=at_sems[sl_])
                S.op("gpsimd", lambda e: e.dma_start(out=wq[:, sl_, 1536:3072], in_=aw_qkv[g_, hp_, :, 1536:3072]),
                     writes=[("wqB", sl_)], dma_sem=atB_sems[sl_])

            for hp in range(8):
                for g in range(3):
                    P, r = GROUPS[g]
                    nbk = 16 // r
                    it = ctr[0]
                    ctr[0] += 1
                    sl = it % 2
                    if it == 0:
                        load_wq(0)
                    load_wq(it + 1)
                    if own:
                        S.op("sync", lambda e, g=g, hp=hp, P=P: e.dma_start(out=kext[:, 0:P], in_=kvK[par_prev, g, hp, :, 0:P]),
                             reads=[("kvK", par_prev, g, hp)], writes=kx(0, P), dma_sem=at_sems[2])
                        vsrc = kvV[par_prev, g, hp, :, 0:r * 256].rearrange("p (s c) -> p s c", s=r)
                        vdst = vext[:, 0:r * (nbk + 1), :].rearrange("p (s n) c -> p s n c", s=r)[:, :, 0, :]
                        S.op("sync", lambda e, vsrc=vsrc, vdst=vdst: e.dma_start(out=vdst, in_=vsrc),
                             reads=[("kvV", par_prev, g, hp)], writes=[("vext", s_ * (nbk + 1)) for s_ in range(r)], dma_sem=at_sems[3])
                        src = bass.AP(wr_dram, (g * 16 + hp * 2) * 384, [[1, 128], [384, 2], [128, 2], [1, 128]])
                        S.op("sync", lambda e, src=src: e.dma_start(out=ebr[:, :, :, :], in_=src),
                             reads=["wr_dram"], writes=["ebr"], dma_sem=at_sems[4])
                        for h in range(2):
                            for c in range(2):
                                rev = bass.AP(ebr.tensor, ebr[:, h, c, :].offset + 127, [list(ebr[:, h, c, :].ap[0]), [-1, 128]])
                                S.op("vector", lambda e, h=h, c=c, rev=rev: e.tensor_copy(out=ebf[:, 1, h, c, :], in_=rev),
                                     reads=["ebr"], writes=["ebf"])
                                if c == 0:
                                    S.op("vector", lambda e, h=h, c=c, rev=rev: e.tensor_scalar(
                                        out=ebf[:, 0, h, c, :], in0=rev, scalar1=flag[:, 0:1], scalar2=None, op0=ALU.mult),
                                         reads=["ebr", "flag"], writes=["ebf"])
                                else:
                                    S.op("vector", lambda e, h=h, c=c, rev=rev: e.tensor_copy(out=ebf[:, 0, h, c, :], in_=rev),
                                         reads=["ebr"], writes=["ebf"])
                    if own:
                        tsubs = MAIN_SUBS
                    else:
                        tsubs = [(SPAN - P, P)] if P <= 512 else MAIN_SUBS
                    state = {"pend": None, "pendb": []}

                    def head(task):
                        which, c0, n = task
                        xk = [k for kc in range(8) for k in subkeys("xn", kc, c0, n)]
                        pq = bank()

                        def mmqk(e, sl=sl, c0=c0, n=n, pq=pq, which=which):
                            for kc in range(8):
                                ins = e.matmul(ps[:, pq, 0:n], lhsT=wq[:, sl, kc * 384 + which * 128:kc * 384 + which * 128 + 128],
                                               rhs=xn[:, kc, c0:c0 + n], start=(kc == 0), stop=(kc == 7))
                            return ins

                        S.op("tensor", mmqk, reads=[("wq", sl), ("wqB", sl)] + xk, writes=[("ps", pq)])
                        j = ctr[2] % 4
                        ctr[2] += 1
                        S.op("scalar", lambda e, j=j, n=n, pq=pq: e.activation(out=sq[:, j, 0:n], in_=ps[:, pq, 0:n], func=AF.Square),
                             reads=[("ps", pq)], writes=[("sq", j)])
                        return (task, pq, j)

                    def tail(st):
                        (which, c0, n), pq, j = st
                        pss = bank()
                        S.op("tensor", lambda e, j=j, n=n, pss=pss: e.matmul(ps[:, pss, 0:n], lhsT=bd32[:, :], rhs=sq[:, j, 0:n], start=True, stop=True),
                             reads=[("sq", j), "bd32"], writes=[("ps", pss)])
                        rr = ctr[3] % 2
                        ctr[3] += 1
                        S.op("scalar", lambda e, rr=rr, n=n, pss=pss: e.activation(out=rstd[:, rr, 0:n], in_=ps[:, pss, 0:n], func=AF.Ln, scale=1.0 / 64, bias=EPS),
                             reads=[("ps", pss)], writes=[("rstd", rr)])
                        S.op("scalar", lambda e, rr=rr, n=n: e.activation(out=rstd[:, rr, 0:n], in_=rstd[:, rr, 0:n], func=AF.Exp, scale=-0.5),
                             reads=[("rstd", rr)], writes=[("rstd", rr)])
                        if which == 1:
                            dst = kext[:, P + c0:P + c0 + n]
                            dkey = kx(P + c0, n)
                        else:
                            dst = qT[:, c0:c0 + n]
                            dkey = subkeys("qT", 0, c0, n)
                        gcol = g * 2 + (1 if which == 1 else 0)
                        S.op("vector", lambda e, rr=rr, n=n, pq=pq, dst=dst, gcol=gcol: e.scalar_tensor_tensor(
                            out=dst, in0=ps[:, pq, 0:n], scalar=gqk[:, gcol:gcol + 1],
                            in1=rstd[:, rr, 0:n], op0=ALU.mult, op1=ALU.mult),
                             reads=[("ps", pq), ("rstd", rr), "gqk"], writes=dkey)

                    def flush_tail():
                        if state["pend"] is not None:
                            tail(state["pend"])
                            state["pend"] = None

                    def proj(task):
                        st = head(task)
                        flush_tail()
                        state["pend"] = st

                    def vgroup(grp):
                        pv = bank()
                        for gi_, (s, nn) in enumerate(grp):
                            m0 = (128 * (nn - 1)) * r + s
                            lo, hi = m0, m0 + 127 * r + 1

                            def mmv(e, sl=sl, m0=m0, pv=pv, gi_=gi_, r=r):
                                for kc in range(8):
                                    ins = e.matmul(ps[:, pv, gi_ * 128:(gi_ + 1) * 128], lhsT=xn[:, kc, m0:m0 + 127 * r + 1:r],
                                                   rhs=wq[:, sl, kc * 384 + 256:kc * 384 + 384], start=(kc == 0), stop=(kc == 7))
                                return ins

                            S.op("tensor", mmv, reads=[("wq", sl), ("wqB", sl)] + [k for kc in range(8) for k in subkeys("xn", kc, lo, hi - lo)], writes=[("ps", pv)])
                        flush_tail()
                        L = len(grp)
                        tixs = [s * (nbk + 1) + nn for (s, nn) in grp]
                        st_ = (tixs[1] - tixs[0]) if L > 1 else 1
                        assert all(tixs[i] == tixs[0] + i * st_ for i in range(L))
                        vsel = vext[:, tixs[0]:tixs[0] + st_ * (L - 1) + 1:st_, :]
                        psv = ps[:, pv, 0:L * 128].rearrange("p (t c) -> p t c", c=128)
                        S.op("vector", lambda e, vsel=vsel, psv=psv: e.tensor_copy(out=vsel[:, :, 0:64], in_=psv[:, :, 0:64]),
                             reads=[("ps", pv)], writes=[("vext", t) for t in tixs])
                        S.op("vector", lambda e, vsel=vsel, psv=psv: e.tensor_copy(out=vsel[:, :, 192:256], in_=psv[:, :, 64:128]),
                             reads=[("ps", pv)], writes=[("vext", t) for t in tixs])

                    def blk_head(s, nn):
                        if bank_ctr[0] % 8 == 7:
                            bank_ctr[0] += 1
                        pS0 = bank()
                        pS1 = bank()
                        assert pS1 == pS0 + 1
                        q0 = (128 * (nn - 1)) * r + s
                        e0, e1 = 128 * (nn - 1) * r, 128 * (nn + 1) * r
                        kkeys = kx(e0, e1 - e0)
                        qkeys = subkeys("qT", 0, 128 * (nn - 1) * r, 128 * r)

                        def mms(e, pS0=pS0, pS1=pS1, q0=q0, r=r, nn=nn, s=s):
                            for h in range(2):
                                for c in range(2):
                                    k0 = (128 * (nn - 1 + c)) * r + s
                                    ins = e.matmul(ps[:, (pS0, pS1)[h], c * 128:(c + 1) * 128],
                                                   lhsT=kext[h * 64:(h + 1) * 64, k0:k0 + 127 * r + 1:r],
                                                   rhs=qT[h * 64:(h + 1) * 64, q0:q0 + 127 * r + 1:r], start=True, stop=True)
                            return ins

                        S.op("tensor", mms, reads=kkeys + qkeys, writes=[("ps", pS0), ("ps", pS1)])
                        j = ctr[1] % 3
                        ctr[1] += 1
                        S.op("scalar", lambda e, j=j, pS0=pS0: e.activation(
                            out=exs[:, j, :].rearrange("p (h x) -> p h x", h=2), in_=ps[:, pS0:pS0 + 2, 0:256], func=AF.Exp, scale=0.125),
                             reads=[("ps", pS0), ("ps", pS1)], writes=[("exs", j)])
                        var = 0 if (nn == 1 and span_kind == "S0") else 1
                        meng = "vector"
                        S.op(meng, lambda e, j=j, var=var: e.tensor_tensor(
                            out=pT[:, j, :], in0=exs[:, j, :], in1=ebf[:, var, :, :, :].rearrange("p h c i -> p (h c i)"), op=ALU.mult),
                             reads=[("exs", j), "ebf"], writes=[("pT", j)])
                        return (s, nn, j, q0)

                    def blk_tail(st):
                        s, nn, j, q0 = st
                        pO = bank()

                        def mmo(e, j=j, pO=pO, nn=nn, s=s, nbk=nbk):
                            for h in range(2):
                                for c in range(2):
                                    tix = s * (nbk + 1) + (nn - 1 + c)
                                    ins = e.matmul(ps[:, pO, h * 128:(h + 1) * 128], lhsT=vext[:, tix, h * 128:(h + 1) * 128],
                                                   rhs=pT[:, j, (h * 2 + c) * 128:(h * 2 + c + 1) * 128], start=(c == 0), stop=(c == 1))
                            return ins

                        S.op("tensor", mmo, reads=[("pT", j), ("vext", s * (nbk + 1) + nn - 1), ("vext", s * (nbk + 1) + nn)], writes=[("ps", pO)])
                        adst = acc[:, :, q0:q0 + 127 * r + 1:r]
                        asrc = ps[:, pO, 0:256].rearrange("p (h i) -> p h i", h=2)
                        if g == 0:
                            S.op("scalar", lambda e, adst=adst, asrc=asrc: e.activation(out=adst, in_=asrc, func=AF.Copy),
                                 reads=[("ps", pO)], writes=["acc"])
                        else:
                            S.op("vector", lambda e, adst=adst, asrc=asrc: e.tensor_tensor(out=adst, in0=asrc, in1=adst, op=ALU.add),
                                 reads=[("ps", pO), "acc"], writes=["acc"])

                    def blocks(lst):
                        for (s, nn) in lst:
                            state["pendb"].append(blk_head(s, nn))
                            if len(state["pendb"]) > 2:
                                blk_tail(state["pendb"].pop(0))

                    def flush_blocks():
                        while state["pendb"]:
                            blk_tail(state["pendb"].pop(0))

                    def spill():
                        if store:
                            S.op("sync", lambda e, g=g, hp=hp, P=P: e.dma_start(out=kvK[par_cur, g, hp, :, 0:P], in_=kext[:, SPAN:SPAN + P]),
                                 reads=kx(SPAN, P), writes=[("kvK", par_cur, g, hp)], dma_sem=sp_sems[it % 6])
                            vdst = kvV[par_cur, g, hp, :, 0:r * 256].rearrange("p (s c) -> p s c", s=r)
                            vsrc = vext[:, 0:r * (nbk + 1), :].rearrange("p (s n) c -> p s n c", s=r)[:, :, nbk, :]
                            S.op("sync", lambda e, vsrc=vsrc, vdst=vdst: e.dma_start(out=vdst, in_=vsrc),
                                 reads=[("vext", s_ * (nbk + 1) + nbk) for s_ in range(r)], writes=[("kvV", par_cur, g, hp)], dma_sem=sp_sems[6 + it % 6])

                    if not own:
                        for (c0, n) in tsubs:
                            proj((1, c0, n))
                        vt = [(s, nbk) for s in range(r)]
                        for ti in range(0, len(vt), 4):
                            vgroup(vt[ti:ti + 4])
                        flush_tail()
                        spill()
                        continue
                    if g < 2:
                        def punit(k):
                            c0, n = MAIN_SUBS[k]
                            proj((1, c0, n))
                            proj((0, c0, n))
                            if g == 0:
                                vgroup([(0, nn) for nn in range(4 * k + 1, 4 * k + 5)])
                            else:
                                vgroup([(s, k + 1) for s in range(4)])

                        def bunit(k):
                            if g == 0:
                                blocks([(0, nn) for nn in range(4 * k + 1, 4 * k + 5)])
                            else:
                                blocks([(s, k + 1) for s in range(4)])
                            bg_pop(3)

                        punit(0)
                        punit(1)
                        bunit(0)
                        punit(2)
                        bunit(1)
                        punit(3)
                        flush_tail()
                        bunit(2)
                        bunit(3)
                        flush_blocks()
                        spill()
                    else:
                        for (c0, n) in MAIN_SUBS:
                            proj((1, c0, n))
                            proj((0, c0, n))
                        vt = [(s, 1) for s in range(r)]
                        for ti in range(0, len(vt), 4):
                            vgroup(vt[ti:ti + 4])
                        flush_tail()
                        for ti in range(0, 16, 4):
                            blocks([(s, 1) for s in range(ti, ti + 4)])
                            bg_pop(3)
                        flush_blocks()
                        spill()
                if not own:
                    continue
                asl = hp % 2
                S.op("gpsimd", lambda e, asl=asl, hp=hp: e.dma_start(out=wao[:, asl, :], in_=aw_out[hp * 128:(hp + 1) * 128, :]),
                     writes=[("wao", asl)], dma_sem=at_sems[7 + asl])
                for q in range(4):
                    cs = slice(q * 512, (q + 1) * 512)
                    rq = q % 2
                    S.op("scalar", lambda e, cs=cs, rq=rq: e.activation(out=rden[0:64, rq, :], in_=acc[64:128, 0, cs], func=AF.Ln), reads=["acc"], writes=[("rden", rq)])
                    S.op("scalar", lambda e, cs=cs, rq=rq: e.activation(out=rden[64:128, rq, :], in_=acc[0:64, 1, cs], func=AF.Ln), reads=["acc"], writes=[("rden", rq)])
                    S.op("scalar", lambda e, rq=rq: e.activation(out=rden[:, rq, :], in_=rden[:, rq, :], func=AF.Exp, scale=-1.0), reads=[("rden", rq)], writes=[("rden", rq)])
                    S.op("vector", lambda e, asl=asl, cs=cs, rq=rq: e.tensor_tensor(out=ao[0:64, asl, cs], in0=acc[0:64, 0, cs], in1=rden[0:64, rq, :], op=ALU.mult),
                         reads=["acc", ("rden", rq)], writes=[("ao", asl, q)])
                    S.op("vector", lambda e, asl=asl, cs=cs, rq=rq: e.tensor_tensor(out=ao[64:128, asl, cs], in0=acc[64:128, 1, cs], in1=rden[64:128, rq, :], op=ALU.mult),
                         reads=["acc", ("rden", rq)], writes=[("ao", asl, q)])
                for qi, (c0, n) in enumerate(MAIN_SUBS):
                    for d in range(8):
                        def oproj(asl=asl, d=d, c0=c0, n=n, qi=qi):
                            py = bank()
                            S.op("tensor", lambda e: e.matmul(
                                ps[:, py, 0:n], lhsT=wao[:, asl, d * 128:(d + 1) * 128], rhs=ao[:, asl, c0:c0 + n], start=True, stop=True),
                                 reads=[("wao", asl), ("ao", asl, qi)], writes=[("ps", py)])
                            S.op("vector", lambda e: e.tensor_tensor(
                                out=hT[:, d, c0:c0 + n], in0=ps[:, py, 0:n], in1=hT[:, d, c0:c0 + n], op=ALU.add),
                                 reads=[("ps", py)] + subkeys("h", d, c0, n), writes=subkeys("h", d, c0, n))
                        bg.append(oproj)
            bg_pop(len(bg))
            S.barrier(dummy[:, :])

        ffn_wi_sems = [new_sem() for _ in range(3)]
        ffn_wo_sems = [[new_sem() for _ in range(6)] for _ in range(2)]
        cv_sems = [new_sem() for _ in range(4)]
        cvB_sems = [new_sem() for _ in range(2)]
        atB_sems = [new_sem() for _ in range(2)]
        sp_sems = [new_sem() for _ in range(12)]
        at_sems = [new_sem() for _ in range(9)]
        io_sem = new_sem()
        out_sem = new_sem()

        hkeys_all = [("h", kc, s) for kc in range(8) for s in range(5)]
        stages = ["l0ffn1", "l0mix", "l0ffn2", "l1ffn1", "l1mix", "l1ffn2"]
        stop_i = stages.index(STOP_AFTER) if STOP_AFTER else len(stages) - 1
        for si, kind in enumerate(("H", "S0", "S1")):
            if kind == "H":
                src = xh.ap().rearrange("(kc p) t -> p kc t", p=128)
                S.op("sync", lambda e, src=src: e.dma_start(out=hT[:, :, :], in_=src), writes=hkeys_all, dma_sem=io_sem)
                subs_all = MAIN_SUBS + [EXT_SUB]
            else:
                o0 = (si - 1) * SPAN
                src = xo[:, o0:o0 + SPAN].rearrange("(kc p) t -> p kc t", p=128)
                S.op("sync", lambda e, src=src: e.dma_start(out=hT[:, :, 0:SPAN], in_=src), writes=hkeys_all, dma_sem=io_sem)
                subs_all = MAIN_SUBS
            ffn(0, 0, subs_all)
            if stop_i >= 1:
                conv_mixer(1, subs_all, kind)
            if stop_i >= 2:
                ffn(1, 2, MAIN_SUBS)
            if stop_i >= 3:
                ffn(2, 3, MAIN_SUBS)
            if stop_i >= 4:
                attn_mixer(4, kind, par_prev=(si + 1) % 2, par_cur=si % 2)
            if kind != "H":
                if stop_i >= 5:
                    ffn(3, 5, MAIN_SUBS)
                o0 = (si - 1) * SPAN
                dst = outT[:, o0:o0 + SPAN].rearrange("(kc p) t -> p kc t", p=128)
                S.op("sync", lambda e, dst=dst: e.dma_start(out=dst, in_=hT[:, :, 0:SPAN]), reads=hkeys_all, writes=[("out", si)], dma_sem=out_sem)
        S.op("sync", lambda e: e.nop(), reads=[("out", 1), ("out", 2)])
        S.finalize(sems)
        S.emit(block)
    return nc


def _bucket_onehot():
    oh = np.zeros((32, 3, 129), np.float32)
    for g, (window, dil) in enumerate(GROUPS):
        for c in range(129):
            step = 128 - c
            dist = step * dil
            if dist < 16:
                b = dist
            else:
                nf = np.float32(max(dist, 1))
                lg = int(np.float32(np.log(nf / np.float32(16)) / np.float32(math.log(2048 / 16)) * np.float32(16)))
                b = min(16 + lg, 31)
            oh[b, g, c] = 1.0
    return oh.reshape(32, 3 * 129)


_CACHE = {}


def _get_program():
    key = (STOP_AFTER, SAME_ENGINE_SYNC)
    if key not in _CACHE:
        _CACHE[key] = build_program()
    return _CACHE[key]


def prepare_inputs(x, norm_ffn1, ffn1_w_in, ffn1_w_out, norm_mix, conv_w_in, conv_w, conv_w_out,
                   attn_w_qkv, attn_q_gain, attn_k_gain, attn_w_out, rel_bias, norm_ffn2, ffn2_w_in, ffn2_w_out):
    f32 = np.float32
    x = np.asarray(x, f32)

    def gl(v):
        return np.asarray(v, f32).reshape(8, 128).T

    gains = np.concatenate([gl(norm_ffn1[0]), gl(norm_mix[0]), gl(norm_ffn2[0]),
                            gl(norm_ffn1[1]), gl(norm_mix[1]), gl(norm_ffn2[1])], axis=1)

    def win_layout(w):
        w = np.asarray(w, f32)
        gate = w[:, :DFF].reshape(8, 128, NB, 128)
        up = w[:, DFF:].reshape(8, 128, NB, 128)
        both = np.stack([gate, up], axis=3)
        return np.ascontiguousarray(both.transpose(2, 1, 0, 3, 4)).reshape(NB, 128, 2048)

    fw_in = np.stack([win_layout(ffn1_w_in[0]), win_layout(ffn2_w_in[0]), win_layout(ffn1_w_in[1]), win_layout(ffn2_w_in[1])])
    fw_out = np.stack([np.asarray(ffn1_w_out[0], f32), np.asarray(ffn2_w_out[0], f32),
                       np.asarray(ffn1_w_out[1], f32), np.asarray(ffn2_w_out[1], f32)])
    cwi = np.asarray(conv_w_in[0], f32).reshape(8, 128, 3, 8, 128)
    cw_in = np.ascontiguousarray(cwi.transpose(3, 1, 0, 2, 4)).reshape(8, 128, 3072)
    cw = np.ascontiguousarray(np.asarray(conv_w[0], f32).reshape(3, 8, 128).transpose(2, 1, 0)).reshape(128, 24)
    awq = np.asarray(attn_w_qkv[0], f32).reshape(8, 128, 3, 3, 8, 128)
    aw_qkv = np.ascontiguousarray(awq.transpose(2, 4, 1, 0, 3, 5)).reshape(3, 8, 128, 3072)
    gqk = np.zeros((128, 6), f32)
    for g in range(3):
        gqk[:, g * 2 + 0] = np.tile(np.asarray(attn_q_gain[0][g], f32), 2)
        gqk[:, g * 2 + 1] = np.tile(np.asarray(attn_k_gain[0][g], f32), 2)
    common = {
        "gains": np.ascontiguousarray(gains), "fw_in": fw_in, "fw_out": fw_out, "cw_in": cw_in, "cw": cw,
        "cw_out": np.ascontiguousarray(np.asarray(conv_w_out[0], f32)), "aw_qkv": aw_qkv,
        "aw_out": np.ascontiguousarray(np.asarray(attn_w_out[0], f32)), "gqk": gqk,
        "relb": np.ascontiguousarray(np.asarray(rel_bias, f32)), "onehot": _bucket_onehot(),
    }
    in_maps = []
    for c in range(NCORES):
        b, half = c // 2, c % 2
        xs = x[b]
        own = xs[half * 4096:(half + 1) * 4096]
        if half == 1:
            halo = np.concatenate([xs[2048:4096], xs[2046:2048]], axis=0)
            fl = 1.0
        else:
            halo = np.concatenate([xs[0:2048], xs[0:2]], axis=0)
            fl = 0.0
        m = dict(common)
        m["xh"] = np.ascontiguousarray(halo.T)
        m["xo"] = np.ascontiguousarray(own.T)
        m["flag"] = np.full((128, 1), fl, f32)
        in_maps.append(m)
    return in_maps


def kernel(**inputs):
    import time, sys
    t0 = time.time()
    in_maps = prepare_inputs(**inputs)
    t1 = time.time()
    nc = _get_program()
    t2 = time.time()
    res = run_bass_kernel_spmd(nc, in_maps, core_ids=list(range(NCORES)))
    t3 = time.time()
    print("kernel(): prep %.1fs build %.1fs run %.1fs" % (t1 - t0, t2 - t1, t3 - t2), file=sys.stderr)
    out = np.empty((4, 8192, D), np.float32)
    for c in range(NCORES):
        b, half = c // 2, c % 2
        out[b, half * 4096:(half + 1) * 4096, :] = res.results[c]["outT"].T
    return out
```
